# Optimizing a Trainium2 kernel written in Bass

```python
import math
import jax, jax.numpy as jnp
from jax import lax
import numpy as np

D_MODEL = 2048
BATCH = 2
SEQ = 4096
DEPTH = 1

HEAD_DIM = 128
N_HEADS_DIL = 8
D_DIL = N_HEADS_DIL * HEAD_DIM
DIL_PATTERNS = ((128, 1), (512, 4), (2048, 16))
BLOCK = 128
N_GROUPS_SG = 4
SG_CHUNK = 128
SG_GROUP_DIM = 128
D_SG = N_GROUPS_SG * SG_GROUP_DIM
N_HEADS_MEM = 4
D_MEM_ATTN = N_HEADS_MEM * HEAD_DIM
N_MEM = 256
D_MIX = D_DIL + D_SG + D_MEM_ATTN
D_IN_PROJ = 3 * D_DIL + 2 * D_SG + D_MEM_ATTN
N_BUCKETS = 32
MAX_DISTANCE = 2048
D_FF = -(-(8 * D_MODEL) // (3 * 256)) * 256
DEEPNORM_ALPHA = (2 * DEPTH) ** 0.25
DEEPNORM_BETA = (8 * DEPTH) ** -0.25
LN_EPS = 1e-5

kernel_name = "hybrid_dilated_sgmlp_memxattn_deepnorm_block"


def layer_norm(x, g, b):
    xf = x.astype(jnp.float32)
    mu = xf.mean(-1, keepdims=True)
    var = jnp.mean(jnp.square(xf - mu), -1, keepdims=True)
    return ((xf - mu) * lax.rsqrt(var + LN_EPS) * g + b).astype(x.dtype)


def t5_bucket(dist):
    max_exact = N_BUCKETS // 2
    d = jnp.maximum(dist, 1).astype(jnp.float32)
    large = max_exact + (jnp.log(d / max_exact) / math.log(MAX_DISTANCE / max_exact)
                         * (N_BUCKETS - max_exact)).astype(jnp.int32)
    large = jnp.minimum(large, N_BUCKETS - 1)
    return jnp.where(dist < max_exact, dist, large)


def dilated_pattern_attn(q, k, v, rel_bias, dilation, n_steps):
    b, h, s, dh = q.shape
    L = s // dilation

    def split(t):
        return t.reshape(b, h, L, dilation, dh).transpose(0, 1, 3, 2, 4)

    qs, ks, vs = split(q), split(k), split(v)
    nb = -(-L // BLOCK)
    lp = nb * BLOCK
    qs = jnp.pad(qs, ((0, 0), (0, 0), (0, 0), (0, lp - L), (0, 0)))
    pad_kv = ((0, 0), (0, 0), (0, 0), (n_steps, lp - L), (0, 0))
    ks = jnp.pad(ks, pad_kv)
    vs = jnp.pad(vs, pad_kv)
    key_idx = jnp.arange(nb)[:, None] * BLOCK + jnp.arange(BLOCK + n_steps)[None, :]
    kb = ks[:, :, :, key_idx]
    vb = vs[:, :, :, key_idx]
    qb = qs.reshape(b, h, dilation, nb, BLOCK, dh)

    steps = jnp.arange(BLOCK)[:, None] + n_steps - jnp.arange(BLOCK + n_steps)[None, :]
    key_sub = key_idx - n_steps
    valid = (steps >= 0) & (steps <= n_steps) & (key_sub[:, None, :] >= 0)
    bias = rel_bias[t5_bucket(jnp.maximum(steps, 0) * dilation)].transpose(2, 0, 1)

    scores = jnp.einsum('bhrnqd,bhrnkd->bhrnqk', qb, kb).astype(jnp.float32) * (dh ** -0.5)
    scores = scores + bias[None, :, None, None].astype(jnp.float32)
    scores = jnp.where(valid[None, None, None], scores, jnp.finfo(jnp.float32).min)
    m = scores.max(-1, keepdims=True)
    e = jnp.exp(scores - m)
    den = e.sum(-1)
    out = jnp.einsum('bhrnqk,bhrnkd->bhrnqd', e, vb.astype(jnp.float32)) / den[..., None]
    lse = m[..., 0] + jnp.log(den)

    out = out.reshape(b, h, dilation, lp, dh)[:, :, :, :L].transpose(0, 1, 3, 2, 4).reshape(b, h, s, dh)
    lse = lse.reshape(b, h, dilation, lp)[:, :, :, :L].transpose(0, 1, 3, 2).reshape(b, h, s)
    return out, lse


def dilated_mixture_attn(q, k, v, rel_bias):
    outs, lses = [], []
    for window, dilation in DIL_PATTERNS:
        o, l = dilated_pattern_attn(q, k, v, rel_bias, dilation, window // dilation)
        outs.append(o)
        lses.append(l)
    w = jax.nn.softmax(jnp.stack(lses, 0), axis=0)
    return jnp.sum(w[..., None] * jnp.stack(outs, 0), axis=0)


def spatial_gating(u, v, ln_g, ln_b, w_s, b_s):
    v = layer_norm(v, ln_g, ln_b)
    b, s, _ = v.shape
    vc = v.reshape(b, s // SG_CHUNK, SG_CHUNK, N_GROUPS_SG, SG_GROUP_DIM)
    mask = jnp.tril(jnp.ones((SG_CHUNK, SG_CHUNK), dtype=bool))
    w = jnp.where(mask[None], w_s, 0)
    mixed = jnp.einsum('gij,bcjgd->bcigd', w, vc) + b_s.T[None, None, :, :, None]
    return u * mixed.reshape(b, s, D_SG)


def memory_cross_attn(q_m, mem, w_kv):
    b, s, _ = q_m.shape
    kv = mem @ w_kv
    k_m, v_m = jnp.split(kv, 2, axis=-1)
    n_mem = mem.shape[1]
    qh = q_m.reshape(b, s, N_HEADS_MEM, HEAD_DIM)
    kh = k_m.reshape(b, n_mem, N_HEADS_MEM, HEAD_DIM)
    vh = v_m.reshape(b, n_mem, N_HEADS_MEM, HEAD_DIM)
    scores = jnp.einsum('bshd,bmhd->bhsm', qh, kh).astype(jnp.float32) * (HEAD_DIM ** -0.5)
    p = jax.nn.softmax(scores, axis=-1)
    o = jnp.einsum('bhsm,bmhd->bshd', p, vh.astype(jnp.float32))
    return o.reshape(b, s, D_MEM_ATTN)


def setup_inputs(seed: int = 0) -> dict:
    key = jax.random.key(seed)
    ks = jax.random.split(key, 20)
    n = jax.random.normal
    f32 = jnp.float32
    return {
        "x": n(ks[0], (BATCH, SEQ, D_MODEL), f32),
        "mem": n(ks[1], (BATCH, N_MEM, D_MODEL), f32),
        "w_in": n(ks[2], (DEPTH, D_MODEL, D_IN_PROJ), f32) * D_MODEL ** -0.5,
        "rel_bias": n(ks[3], (N_BUCKETS, N_HEADS_DIL), f32) * 0.5,
        "sg_ln_g": 1.0 + 0.02 * n(ks[4], (DEPTH, D_SG), f32),
        "sg_ln_b": 0.02 * n(ks[5], (DEPTH, D_SG), f32),
        "w_spatial": n(ks[6], (DEPTH, N_GROUPS_SG, SG_CHUNK, SG_CHUNK), f32) * SG_CHUNK ** -0.5,
        "b_spatial": 1.0 + 0.02 * n(ks[7], (DEPTH, N_GROUPS_SG, SG_CHUNK), f32),
        "w_mem_kv": n(ks[8], (DEPTH, D_MODEL, 2 * D_MEM_ATTN), f32) * D_MODEL ** -0.5,
        "w_out": n(ks[9], (DEPTH, D_MIX, D_MODEL), f32) * (D_MIX ** -0.5 * DEEPNORM_BETA),
        "ln1_g": 1.0 + 0.02 * n(ks[10], (DEPTH, D_MODEL), f32),
        "ln1_b": 0.02 * n(ks[11], (DEPTH, D_MODEL), f32),
        "w_gate": n(ks[12], (DEPTH, D_MODEL, D_FF), f32) * D_MODEL ** -0.5,
        "w_up": n(ks[13], (DEPTH, D_MODEL, D_FF), f32) * D_MODEL ** -0.5,
        "w_down": n(ks[14], (DEPTH, D_FF, D_MODEL), f32) * (D_FF ** -0.5 * DEEPNORM_BETA),
        "ln2_g": 1.0 + 0.02 * n(ks[15], (DEPTH, D_MODEL), f32),
        "ln2_b": 0.02 * n(ks[16], (DEPTH, D_MODEL), f32),
    }


def reference(x, mem, w_in, rel_bias, sg_ln_g, sg_ln_b, w_spatial, b_spatial, w_mem_kv, w_out,
              ln1_g, ln1_b, w_gate, w_up, w_down, ln2_g, ln2_b):
    b, s, _ = x.shape
    cuts = [D_DIL, 2 * D_DIL, 3 * D_DIL, 3 * D_DIL + D_SG, 3 * D_DIL + 2 * D_SG]
    for l in range(DEPTH):
        hcat = x @ w_in[l]
        q_d, k_d, v_d, u_sg, v_sg, q_m = jnp.split(hcat, cuts, axis=-1)

        def heads(t):
            return t.reshape(b, s, N_HEADS_DIL, HEAD_DIM).transpose(0, 2, 1, 3)

        o_dil = dilated_mixture_attn(heads(q_d), heads(k_d), heads(v_d), rel_bias)
        o_dil = o_dil.transpose(0, 2, 1, 3).reshape(b, s, D_DIL).astype(x.dtype)

        o_sg = spatial_gating(jax.nn.gelu(u_sg), jax.nn.gelu(v_sg), sg_ln_g[l], sg_ln_b[l],
                              w_spatial[l], b_spatial[l]).astype(x.dtype)

        o_mem = memory_cross_attn(q_m, mem, w_mem_kv[l]).astype(x.dtype)

        mix = jnp.concatenate([o_dil, o_sg, o_mem], axis=-1) @ w_out[l]
        x = layer_norm(DEEPNORM_ALPHA * x + mix, ln1_g[l], ln1_b[l])

        f = (jax.nn.silu(x @ w_gate[l]) * (x @ w_up[l])) @ w_down[l]
        x = layer_norm(DEEPNORM_ALPHA * x + f, ln2_g[l], ln2_b[l])
    return x
```

```python
import contextlib
import math
import numpy as np
import concourse.bass as bass
import concourse.mybir as mybir
from concourse.bass_utils import run_bass_kernel_spmd

F32 = mybir.dt.float32
BF16 = mybir.dt.bfloat16
AF = mybir.ActivationFunctionType
ALU = mybir.AluOpType

K = 1024
D = 2048
NF = 44
ALPHA = float(2.0 ** 0.25)
SCALE = float(128.0 ** -0.5)
LN_EPS = 1e-5
TZROW = 128 * 385 + 128
SBUF_BASE = 16512
ENGS = ("pe", "act", "dve", "pool", "sp")
VOWN, VR1H, VR4H, V16A = 0, 1, 2, 3


class Sched:
    def __init__(self, nc, stack):
        self.nc = nc
        self.stack = stack
        self.prog = {e: [] for e in ENGS}
        self.sem_h = {}
        self.count = {}
        self.known = {e: {} for e in ENGS}
        self.res = {}
        for e in ("pe", "act", "dve", "pool"):
            self._sem("E_" + e)
        self.out_events = []

    def _sem(self, name):
        if name not in self.sem_h:
            self.sem_h[name] = self.stack.enter_context(self.nc.semaphore(name))
            self.count[name] = 0
        return self.sem_h[name]

    def _r(self, name):
        if name not in self.res:
            self.res[name] = {"w": None, "r": []}
        return self.res[name]

    def _waits(self, eng, reads, writes):
        need = {}

        def add(ev):
            if ev is None:
                return
            s, v = ev
            if need.get(s, 0) < v:
                need[s] = v

        for r in reads:
            add(self._r(r)["w"])
        for w in writes:
            rr = self._r(w)
            add(rr["w"])
            for ev in rr["r"]:
                add(ev)
        out = []
        for s, v in need.items():
            if self.known[eng].get(s, 0) >= v:
                continue
            self.known[eng][s] = v
            out.append((s, v))
        return out

    def _commit(self, ev, reads, writes):
        for r in reads:
            self._r(r)["r"].append(ev)
        for w in writes:
            rr = self._r(w)
            rr["w"] = ev
            rr["r"] = []

    def op(self, eng, fn, reads=(), writes=()):
        waits = self._waits(eng, reads, writes)
        s = "E_" + eng
        self.count[s] += 1
        ev = (s, self.count[s])
        self.prog[eng].append((waits, fn, s, 1))
        self._commit(ev, reads, writes)
        return ev

    def dma(self, queue, fns, sem, reads=(), writes=(), is_output=False):
        if not isinstance(fns, (list, tuple)):
            fns = [fns]
        self._sem(sem)
        waits = self._waits(queue, reads, writes)
        for i, fn in enumerate(fns):
            self.count[sem] += 16
            self.prog[queue].append((waits if i == 0 else [], fn, sem, 16))
        ev = (sem, self.count[sem])
        self._commit(ev, reads, writes)
        if is_output:
            self.out_events.append(ev)
        return ev

    def alias(self, old, new):
        evs = []
        for o in old:
            rr = self._r(o)
            if rr["w"] is not None:
                evs.append(rr["w"])
            evs.extend(rr["r"])
        for n in new:
            self._r(n)["r"].extend(evs)

    def barrier(self):
        snap = dict(self.count)
        for e in ENGS:
            waits = []
            for s, v in snap.items():
                if v > 0 and self.known[e].get(s, 0) < v:
                    self.known[e][s] = v
                    waits.append((s, v))
            self.prog[e].append((waits, None, None, 0))
        self.res = {}

    def emit(self):
        nc = self.nc
        fin = {}
        for s, v in self.out_events:
            fin[s] = max(fin.get(s, 0), v)
        with nc.Block() as block:
            def run(engname, eng):
                for waits, fn, s, inc in self.prog[engname]:
                    for ws, wv in waits:
                        eng.wait_ge(self.sem_h[ws], wv)
                    if fn is not None:
                        ins = fn(eng)
                        ins.then_inc(self.sem_h[s], inc)

            @block.tensor
            def _(e):
                run("pe", e)

            @block.scalar
            def _(e):
                run("act", e)

            @block.vector
            def _(e):
                run("dve", e)

            @block.gpsimd
            def _(e):
                run("pool", e)

            @block.sync
            def _(e):
                run("sp", e)
                for s, v in fin.items():
                    e.wait_ge(self.sem_h[s], v)


def build_program():
    nc = bass.Bass("TRN2", target_bir_lowering=False)

    def din(name, shape):
        return nc.dram_tensor(name, shape, F32, kind="ExternalInput")

    d_xTh = din("xTh", [128, 16 * 2048])
    d_xTo = din("xTo", [128, 16 * 1024])
    d_xo = din("xo", [1024, 2048])
    d_memT = din("memT", [128, 16 * 256])
    d_valid = din("valid", [128, 4])
    d_winh = din("winh", [8, 128, 16 * 384])
    d_winsg = din("winsg", [6, 128, 16 * 256])
    d_wkv = din("wkv", [4, 128, 16 * 256])
    d_wout = din("wout", [8, 128, 16 * 256])
    d_wgu = din("wgu", [NF, 128, 16 * 256])
    d_wd = din("wd", [8, 128, NF * 256])
    d_wsT = din("wsT", [128, 4 * 128])
    d_tril = din("tril", [128, 128])
    d_ident = din("ident", [128, 128])
    d_bsp = din("bsp", [1, 512])
    d_sgg = din("sgg", [1, 512])
    d_sgb = din("sgb", [1, 512])
    d_ln1g = din("ln1g", [1, 2048])
    d_ln1b = din("ln1b", [1, 2048])
    d_ln2g = din("ln2g", [1, 2048])
    d_ln2b = din("ln2b", [1, 2048])
    d_rel = din("rel", [32, 8])
    d_oh = din("oh", [33, 3 * 384])
    d_out = nc.dram_tensor("out", [1024, 2048], F32, kind="ExternalOutput")
    d_tz = nc.dram_tensor("tz", [1, 24 * TZROW], F32)

    def bcast_rows(handle, n):
        return bass.AP(handle, 0, [[0, 128], [1, n]])

    def A(name, shape, dt, off):
        return nc.alloc_sbuf_tensor_at(name, shape, dt, offset=SBUF_BASE + int(off))

    with contextlib.ExitStack() as st:
        S = Sched(nc, st)
        xTh = A("xTh", [128, 16, 2048], BF16, 0)
        xTo = A("xTo", [128, 16, 1024], BF16, 64 * K)
        ocatT = A("ocatT", [128, 16, 1024], BF16, 96 * K)
        wsl = [A("wsl0", [128, 16, 384], BF16, 128 * K), A("wsl1", [128, 16, 384], BF16, 140 * K)]
        psl = [A(f"psl{i}", [128, 16, 256], BF16, (128 + 8 * i) * K) for i in range(3)]
        QT = A("QT", [128, 1024], BF16, 152 * K)
        KT = A("KT", [128, 3072], BF16, 154 * K)
        VT = A("VT", [128, 3072], BF16, 160 * K)
        QT1 = A("QT1", [128, 1024], BF16, 112 * K)
        KT1 = A("KT1", [128, 3072], BF16, 114 * K)
        VT1 = A("VT1", [128, 3072], BF16, 120 * K)
        Varr = A("Varr", [128, 53, 128], BF16, 166 * K)
        Tt = [A("Tt0", [128, 3, 256], F32, 180 * K), A("Tt1", [128, 3, 256], F32, 183 * K)]
        acc = A("acc", [128, 2, 1024], F32, 186 * K)
        stt = [A("st0", [128, 512], F32, 194 * K), A("st1", [128, 512], F32, 196 * K)]
        rdh = A("rdh", [128, 1024], F32, 194 * K)
        pTt = [A("pT0", [128, 512], BF16, 198 * K), A("pT1", [128, 512], BF16, 199 * K)]
        pTm = A("pTm", [128, 2, 512], BF16, 198 * K)
        identb = A("identb", [128, 128], BF16, 200 * K)
        vones = A("vones", [128, 4, 128], BF16, 200 * K + 256)
        validf = A("validf", [128, 4], F32, 201 * K + 256)
        relx = A("relx", [33, 8], F32, 201 * K + 512)
        identf = A("identf", [128, 128], F32, 202 * K)
        Gs = A("Gs", [128, 24, 384], F32, 152 * K)
        relrep = A("relrep", [33, 8, 128], F32, 190 * K)
        ohs = A("ohs", [33, 3, 384], F32, 194 * K)
        uT = A("uT", [128, 4, 1024], F32, 0)
        vg = A("vg", [128, 8, 512], F32, 16 * K)
        vln = A("vln", [128, 8, 512], BF16, 32 * K)
        memTb = A("memTb", [128, 16, 256], BF16, 40 * K)
        KmT = A("KmT", [128, 4, 256], BF16, 48 * K)
        Vm = A("Vm", [128, 2, 512], BF16, 50 * K)
        qmT = A("qmT", [128, 4, 1024], BF16, 52 * K)
        sgg = A("sgg", [128, 512], F32, 152 * K)
        sgb = A("sgb", [128, 512], F32, 154 * K)
        bsbc = A("bsbc", [128, 4, 128], F32, 156 * K)
        wsTf = A("wsTf", [128, 4, 128], F32, 158 * K)
        wsTm = A("wsTm", [128, 4, 128], BF16, 160 * K)
        trilf = A("trilf", [128, 128], F32, 161 * K)
        stats_b = A("stats_b", [128, 8, 6], F32, 162 * K)
        mv_b = A("mv_b", [128, 8, 4], F32, 162 * K + 256)
        rdm = A("rdm", [128, 512], F32, 164 * K)
        tmpA = A("tmpA", [128, 4, 128], F32, 166 * K)
        pTm2 = A("pTm2", [128, 2, 512], BF16, 194 * K)
        rdm2 = A("rdm2", [128, 512], F32, 196 * K)
        x1buf = A("x1buf", [128, 8, 2048], F32, 0)
        x1T = A("x1T", [128, 16, 1024], BF16, 64 * K)
        wo = [A("wo0", [128, 16, 256], BF16, 168 * K), A("wo1", [128, 16, 256], BF16, 176 * K)]
        ln1g = A("ln1g", [128, 2048], F32, 144 * K)
        ln1b = A("ln1b", [128, 2048], F32, 152 * K)
        xbs = [A("xb0", [128, 2048], BF16, 160 * K), A("xb1", [128, 2048], BF16, 164 * K)]
        stats_c = A("stats_c", [128, 8, 8, 6], F32, 128 * K)
        mv_c = A("mv_c", [128, 8, 4], F32, 130 * K)
        hT = A("hT", [128, NF, 1024], BF16, 96 * K)
        gu = [A("gu0", [128, 16, 256], BF16, 184 * K), A("gu1", [128, 16, 256], BF16, 192 * K)]
        sgt = [A("sgt0", [128, 512], F32, 200 * K), A("sgt1", [128, 512], F32, 202 * K)]
        wdb = [A("wdA", [128, NF, 256], BF16, 64 * K), A("wdB", [128, NF, 256], BF16, 184 * K)]
        ln2g = A("ln2g", [128, 2048], F32, 64 * K)
        ln2b = A("ln2b", [128, 2048], F32, 72 * K)
        stats_d = A("stats_d", [128, 8, 8, 6], F32, 86 * K)
        mv_d = A("mv_d", [128, 8, 4], F32, 88 * K)

        psf = [st.enter_context(nc.psum_tensor(f"psf{i}", [128, 512], F32)) for i in range(6)]
        psb = [st.enter_context(nc.psum_tensor(f"psb{i}", [128, 1024], BF16)) for i in range(2)]
        bank_ctr = [0]
        bbank_ctr = [0]

        def nextbank():
            b = bank_ctr[0] % 6
            bank_ctr[0] += 1
            return b

        def nextbbank():
            b = bbank_ctr[0] % 2
            bbank_ctr[0] += 1
            return b

        evac_ctr = [0]
        st_ctr = [0]

        def evac_copy(out_ap, in_ap, reads, writes):
            evac_ctr[0] += 1
            if evac_ctr[0] % 2 == 0:
                S.op("act", lambda e: e.copy(out=out_ap, in_=in_ap), reads=reads, writes=writes)
            else:
                S.op("dve", lambda e: e.tensor_copy(out=out_ap, in_=in_ap), reads=reads, writes=writes)

        def cast_dma(out_ap, in_ap):
            return lambda e: e.dma_start(out=out_ap, in_=in_ap, max_dma_last_dim=4096)

        def ln_A(stats_flat, mv, stat_names, mn):
            S.op("dve", lambda e: e.bn_aggr(out=mv[:, 0:2], in_=stats_flat), reads=stat_names, writes=[mn])
            S.op("dve", lambda e: e.tensor_scalar_add(out=mv[:, 2:3], in0=mv[:, 1:2], scalar1=LN_EPS), reads=[mn], writes=[mn])
            S.op("act", lambda e: e.sqrt(out=mv[:, 2:3], in_=mv[:, 2:3]), reads=[mn], writes=[mn])
            S.op("dve", lambda e: e.reciprocal(out=mv[:, 2:3], in_=mv[:, 2:3]), reads=[mn], writes=[mn])

        def ln_B(buf_ap, mv, g_ap, b_ap, res_names, mn, out_ap=None, out_names=None):
            S.op("dve", lambda e: e.scalar_tensor_tensor(out=buf_ap, in0=buf_ap, scalar=mv[:, 0:1], in1=g_ap,
                                                         op0=ALU.subtract, op1=ALU.mult),
                 reads=list(res_names) + [mn, "ln_g"], writes=res_names)
            o = buf_ap if out_ap is None else out_ap
            wn = res_names if out_ap is None else out_names
            S.op("dve", lambda e: e.scalar_tensor_tensor(out=o, in0=buf_ap, scalar=mv[:, 2:3], in1=b_ap,
                                                         op0=ALU.mult, op1=ALU.add),
                 reads=list(res_names) + [mn, "ln_b"], writes=wn)

        S.dma("pool", cast_dma(wsl[0][:, :, :], d_winh.ap()[0].rearrange("p (k c) -> p k c", c=384)), "s_wsl0", writes=["wsl0"])
        S.dma("pool", [cast_dma(xTo[:, 8 * i:8 * i + 8, :], d_xTo.ap().rearrange("p (k t) -> p k t", t=1024)[:, 8 * i:8 * i + 8, :])
                       for i in range(2)], "s_xTo", writes=["xTo"])
        for tb in range(4):
            S.dma("pool", cast_dma(xTh[:, :, tb * 512:(tb + 1) * 512],
                                   d_xTh.ap().rearrange("p (k t) -> p k t", t=2048)[:, :, tb * 512:(tb + 1) * 512]),
                  f"s_xTh{tb}", writes=[f"xTh{tb}"])

        S.dma("sp", [lambda e: e.dma_start(out=identf[:, :], in_=d_ident.ap()),
                     lambda e: e.dma_start(out=validf[:, :], in_=d_valid.ap()),
                     lambda e: e.dma_start(out=relx[0:32, :], in_=d_rel.ap()),
                     lambda e: e.dma_start(out=ohs[:, :, :], in_=d_oh.ap().rearrange("p (r u) -> p r u", u=384))],
              "s_c0", writes=["identf", "validf", "relx", "ohs"])
        S.op("dve", lambda e: e.memset(relx[32:33, :], -30000.0), writes=["relx32"])
        S.op("dve", lambda e: e.tensor_copy(out=identb[:, :], in_=identf[:, :]), reads=["identf"], writes=["identb"])
        S.op("dve", lambda e: e.tensor_copy(out=vones[:, :, :], in_=validf[:, :].unsqueeze(2).broadcast_to([128, 4, 128])),
             reads=["validf"], writes=["vones"])
        S.op("dve", lambda e: e.tensor_copy(out=relrep[:, :, :], in_=relx[:, :].unsqueeze(2).broadcast_to([33, 8, 128])),
             reads=["relx", "relx32"], writes=["relrep"])
        for h in range(8):
            for r in range(3):
                idx = h * 3 + r
                b = nextbank()
                S.op("pe", lambda e, b=b, h=h, r=r: e.matmul(psf[b][:, 0:384], lhsT=relrep[:, h, :], rhs=ohs[:, r, :],
                                                          start=True, stop=True),
                     reads=["relrep", "ohs"], writes=[f"ps{b}"])
                evac_copy(Gs[:, idx, :], psf[b][:, 0:384], [f"ps{b}"], [f"Gs{idx}"])
        S.dma("sp", [(lambda e, idx=idx: e.dma_start(out=bass.AP(d_tz, idx * TZROW, [[385, 128], [1, 384]]), in_=Gs[:, idx, :]))
                     for idx in range(24)], "s_tz", reads=[f"Gs{i}" for i in range(24)], writes=["tz"])
        S.alias([f"Gs{i}" for i in range(24)] + ["relrep", "ohs"],
                ["QT0", "KT0", "VT0", "Varr", "Tt0", "Tt1", "acc", "st0", "st1", "pT0", "pT1"])

        def xT_block(k, tb):
            if tb < 4:
                return xTh[:, k, tb * 512:(tb + 1) * 512]
            return xTo[:, k, (tb - 4) * 512:(tb - 3) * 512]

        R1 = lambda j: j
        R4 = lambda c, m: 9 + 3 * c + m
        R16A = lambda c: 21 + c
        R16B = lambda c: 37 + c
        tile_cols = []
        for j in range(9):
            tile_cols.append((1920 + 128 * j, 1, 128))
        for c in range(4):
            for m in range(3):
                tile_cols.append((1536 + c + 512 * m, 4, 128))
        for c in range(16):
            tile_cols.append((c, 16, 128))
        for c in range(16):
            tile_cols.append((2048 + c, 16, 64))

        def cols(t, start, step, n):
            return t[:, start:start + step * (n - 1) + 1:step]

        QTs, KTs, VTs = [QT, QT1], [KT, KT1], [VT, VT1]

        def load_head_w(h):
            S.dma("pool", cast_dma(wsl[h % 2][:, :, :], d_winh.ap()[h].rearrange("p (k c) -> p k c", c=384)),
                  f"s_wsl{h % 2}", writes=[f"wsl{h % 2}"])

        def proj_groups(h):
            sl = wsl[h % 2]
            sln = f"wsl{h % 2}"
            hb = h % 2
            out = []

            def mk(which, tb):
                def g():
                    b = nextbank()

                    def f(e):
                        ins = None
                        for k in range(16):
                            ins = e.matmul(psf[b][:, :], lhsT=sl[:, k, which * 128:(which + 1) * 128], rhs=xT_block(k, tb),
                                           start=(k == 0), stop=(k == 15))
                        return ins
                    S.op("pe", f, reads=[sln, "xTo" if tb >= 4 else f"xTh{tb}"], writes=[f"ps{b}"])
                    if which == 0:
                        o = QTs[hb][:, (tb - 4) * 512:(tb - 3) * 512]
                        S.op("act", lambda e: e.mul(out=o, in_=psf[b][:, :], mul=SCALE), reads=[f"ps{b}"], writes=[f"QT{hb}"])
                    else:
                        dst = KTs[hb] if which == 1 else VTs[hb]
                        evac_copy(dst[:, tb * 512:(tb + 1) * 512], psf[b][:, :], [f"ps{b}"], [("KT" if which == 1 else "VT") + str(hb)])
                return g
            for which, tb in ((0, 4), (0, 5), (1, 4), (1, 5), (2, 4), (2, 5), (1, 0), (2, 0), (1, 1), (2, 1), (1, 2), (2, 2), (1, 3), (2, 3)):
                out.append(mk(which, tb))
            return out

        def plan_transposes(h):
            VTh = VTs[h % 2]
            vn = f"VT{h % 2}"
            for g0 in range(0, 53, 8):
                g1 = min(g0 + 8, 53)
                bb = nextbbank()

                def f(e, g0=g0, g1=g1, bb=bb):
                    ins = None
                    for ti in range(g0, g1):
                        s0, stp, n = tile_cols[ti]
                        ins = e.transpose(out=psb[bb][0:n, (ti - g0) * 128:(ti - g0 + 1) * 128], in_=cols(VTh, s0, stp, n),
                                          identity=identb[:, :])
                    return ins
                S.op("pe", f, reads=[vn, "identb"], writes=[f"pb{bb}"])
                nA = max(0, min(g1, 37) - g0)
                nB = (g1 - g0) - nA
                if nA > 0:
                    evac_copy(Varr[:, g0:g0 + nA, :], psb[bb][:, 0:nA * 128].rearrange("p (t d) -> p t d", d=128),
                              [f"pb{bb}"], ["Varr"])
                if nB > 0:
                    evac_copy(Varr[0:64, g0 + nA:g1, :], psb[bb][0:64, nA * 128:(g1 - g0) * 128].rearrange("p (t d) -> p t d", d=128),
                              [f"pb{bb}"], ["Varr"])

        def attn_units(h):
            hb = h % 2
            QTh, KTh = QTs[hb], KTs[hb]
            qn, kn = f"QT{hb}", f"KT{hb}"
            TT = Tt[hb]
            ttn = f"Tt{hb}"
            units = []

            def std_unit(kcur, kprev, qaps, vcur, vprev, vvalid_prev, acc_ap, first, r):
                ctx = {}

                def plan_S():
                    bs = nextbank()
                    sv = psf[bs][:, :].rearrange("p (b t q) -> p b t q", b=2, t=2, q=128)

                    def fs(e):
                        ins = None
                        for blk in range(2):
                            e.matmul(sv[:, blk, 0, :], lhsT=kcur[blk], rhs=qaps[blk], start=True, stop=True)
                            ins = e.matmul(sv[:, blk, 1, :], lhsT=kprev[blk], rhs=qaps[blk], start=True, stop=True)
                        return ins
                    S.op("pe", fs, reads=[kn, qn], writes=[f"ps{bs}"])
                    si = st_ctr[0] % 2
                    st_ctr[0] += 1
                    ctx["si"] = si
                    S.op("dve", lambda e: e.tensor_tensor(out=stt[si][:, :].rearrange("p (b j) -> p b j", b=2),
                                                          in0=psf[bs][:, :].rearrange("p (b j) -> p b j", b=2),
                                                          in1=TT[:, r, :].unsqueeze(1).broadcast_to([128, 2, 256]), op=ALU.add),
                         reads=[f"ps{bs}", ttn], writes=[f"st{si}"])
                    S.op("act", lambda e: e.activation(out=pTt[si][:, :], in_=stt[si][:, :], func=AF.Exp),
                         reads=[f"st{si}"], writes=[f"pT{si}"])

                def plan_PV():
                    si = ctx["si"]
                    bo = nextbank()
                    ov = psf[bo][:, :].rearrange("p (a q) -> p a q", a=2)

                    def fo(e):
                        ins = None
                        for blk in range(2):
                            pc = pTt[si][:, blk * 256:blk * 256 + 128]
                            pp = pTt[si][:, blk * 256 + 128:blk * 256 + 256]
                            e.matmul(ov[:, 0, blk * 128:(blk + 1) * 128], lhsT=vcur[blk], rhs=pc, start=True, stop=False)
                            e.matmul(ov[:, 0, blk * 128:(blk + 1) * 128], lhsT=vprev[blk], rhs=pp, start=False, stop=True)
                            e.matmul(ov[:, 1, blk * 128:(blk + 1) * 128], lhsT=vones[:, VOWN, :], rhs=pc, start=True, stop=False)
                            ins = e.matmul(ov[:, 1, blk * 128:(blk + 1) * 128], lhsT=vones[:, vvalid_prev[blk], :], rhs=pp,
                                           start=False, stop=True)
                        return ins
                    S.op("pe", fo, reads=[f"pT{si}", "Varr", "vones"], writes=[f"ps{bo}"])
                    if first:
                        evac_copy(acc_ap, ov, [f"ps{bo}"], ["acc"])
                    else:
                        S.op("dve", lambda e: e.tensor_tensor(out=acc_ap, in0=ov, in1=acc_ap, op=ALU.add),
                             reads=[f"ps{bo}", "acc"], writes=["acc"])
                units.append((plan_S, plan_PV))

            for pr in range(4):
                ns = (2 * pr, 2 * pr + 1)
                std_unit(
                    kcur=[KTh[:, 2048 + 128 * n:2048 + 128 * n + 128] for n in ns],
                    kprev=[KTh[:, 1920 + 128 * n:1920 + 128 * n + 128] for n in ns],
                    qaps=[QTh[:, 128 * n:128 * n + 128] for n in ns],
                    vcur=[Varr[:, R1(n + 1), :] for n in ns],
                    vprev=[Varr[:, R1(n), :] for n in ns],
                    vvalid_prev=[VR1H if n == 0 else VOWN for n in ns],
                    acc_ap=acc[:, :, 256 * pr:256 * pr + 256], first=True, r=0)
            for c in range(4):
                ns = (0, 1)
                std_unit(
                    kcur=[cols(KTh, 1536 + c + 512 * (n + 1), 4, 128) for n in ns],
                    kprev=[cols(KTh, 1536 + c + 512 * n, 4, 128) for n in ns],
                    qaps=[cols(QTh, c + 512 * n, 4, 128) for n in ns],
                    vcur=[Varr[:, R4(c, n + 1), :] for n in ns],
                    vprev=[Varr[:, R4(c, n), :] for n in ns],
                    vvalid_prev=[VR4H if n == 0 else VOWN for n in ns],
                    acc_ap=acc[:, :, c:c + 4 * 255 + 1:4], first=False, r=1)

            def r16_unit(cg):
                ctx = {}

                def plan_S():
                    bs = nextbank()
                    sv = psf[bs][:, :].rearrange("p (a c i) -> p a c i", a=2, c=4, i=64)

                    def fs(e):
                        ins = None
                        for ci in range(4):
                            c = 4 * cg + ci
                            q = cols(QTh, c, 16, 64)
                            e.matmul(sv[:, 0, ci, :], lhsT=cols(KTh, c, 16, 128), rhs=q, start=True, stop=True)
                            ins = e.matmul(sv[0:64, 1, ci, :], lhsT=cols(KTh, 2048 + c, 16, 64), rhs=q, start=True, stop=True)
                        return ins
                    S.op("pe", fs, reads=[kn, qn], writes=[f"ps{bs}"])
                    si = st_ctr[0] % 2
                    st_ctr[0] += 1
                    ctx["si"] = si
                    stv = stt[si][:, :].rearrange("p (a c i) -> p a c i", a=2, c=4, i=64)
                    ptv = pTt[si][:, :].rearrange("p (a c i) -> p a c i", a=2, c=4, i=64)
                    ctx["ptv"] = ptv
                    S.op("dve", lambda e: e.tensor_tensor(out=stv[:, 0, :, :], in0=sv[:, 0, :, :],
                                                          in1=TT[:, 2, 128:192].unsqueeze(1).broadcast_to([128, 4, 64]), op=ALU.add),
                         reads=[f"ps{bs}", ttn], writes=[f"st{si}"])
                    S.op("dve", lambda e: e.tensor_tensor(out=stv[0:64, 1, :, :], in0=sv[0:64, 1, :, :],
                                                          in1=TT[0:64, 2, 0:64].unsqueeze(1).broadcast_to([64, 4, 64]), op=ALU.add),
                         reads=[f"ps{bs}", ttn, f"st{si}"], writes=[f"st{si}"])
                    S.op("act", lambda e: e.activation(out=ptv[:, 0, :, :], in_=stv[:, 0, :, :], func=AF.Exp),
                         reads=[f"st{si}"], writes=[f"pT{si}"])
                    S.op("act", lambda e: e.activation(out=ptv[0:64, 1, :, :], in_=stv[0:64, 1, :, :], func=AF.Exp),
                         reads=[f"st{si}", f"pT{si}"], writes=[f"pT{si}"])

                def plan_PV():
                    si = ctx["si"]
                    ptv = ctx["ptv"]
                    bo = nextbank()
                    ov = psf[bo][:, :].rearrange("p (a c i) -> p a c i", a=2, c=4, i=64)

                    def fo(e):
                        ins = None
                        for ci in range(4):
                            c = 4 * cg + ci
                            e.matmul(ov[:, 0, ci, :], lhsT=Varr[:, R16A(c), :], rhs=ptv[:, 0, ci, :], start=True, stop=False)
                            e.matmul(ov[:, 0, ci, :], lhsT=Varr[0:64, R16B(c), :], rhs=ptv[0:64, 1, ci, :], start=False, stop=True)
                        e.matmul(ov[:, 1, :, :], lhsT=vones[:, V16A, :], rhs=ptv[:, 0, :, :], start=True, stop=False)
                        ins = e.matmul(ov[:, 1, :, :], lhsT=vones[0:64, VOWN, :], rhs=ptv[0:64, 1, :, :], start=False, stop=True)
                        return ins
                    S.op("pe", fo, reads=[f"pT{si}", "Varr", "vones"], writes=[f"ps{bo}"])
                    accv = acc[:, :, :].rearrange("p a (i c) -> p a c i", c=16)[:, :, 4 * cg:4 * cg + 4, :]
                    S.op("dve", lambda e: e.tensor_tensor(out=accv, in0=ov, in1=accv, op=ALU.add),
                         reads=[f"ps{bo}", "acc"], writes=["acc"])
                units.append((plan_S, plan_PV))
            for cg in range(4):
                r16_unit(cg)
            return units

        def load_T(h):
            TT = Tt[h % 2]
            S.dma("sp", [(lambda e, r=r: e.dma_start(out=TT[:, r, :],
                                                     in_=bass.AP(d_tz, (h * 3 + r) * TZROW + 127, [[384, 128], [1, 256]])))
                         for r in range(3)], f"s_tt{h % 2}", reads=["tz"], writes=[f"Tt{h % 2}"])

        sgsl = lambda i: (psl[i % 3][:, :, :], f"psl{i % 3}")
        piece_ctr = [0]

        def load_piece(src_ap):
            i = piece_ctr[0]
            piece_ctr[0] += 1
            ap, nm = sgsl(i)
            S.dma("pool", cast_dma(ap, src_ap.rearrange("p (k c) -> p k c", c=256)), f"s_{nm}", writes=[nm])
            return ap, nm

        def uv_groups():
            out = []
            for pi in range(2):
                hold = {}
                for gg in range(2):
                    for tb in range(2):
                        def g(pi=pi, gg=gg, tb=tb, hold=hold):
                            if gg == 0 and tb == 0:
                                hold["p"] = load_piece(d_winsg.ap()[pi])
                            ap, nm = hold["p"]
                            b = nextbank()

                            def f(e):
                                ins = None
                                for k in range(16):
                                    ins = e.matmul(psf[b][:, :], lhsT=ap[:, k, gg * 128:(gg + 1) * 128], rhs=xTo[:, k, tb * 512:(tb + 1) * 512],
                                                   start=(k == 0), stop=(k == 15))
                                return ins
                            S.op("pe", f, reads=[nm, "xTo"], writes=[f"ps{b}"])
                            evac_copy(uT[:, 2 * pi + gg, tb * 512:(tb + 1) * 512], psf[b][:, :], [f"ps{b}"], ["uT"])
                        out.append(g)
            for pi in range(2):
                hold = {}
                for t in range(8):
                    def g(pi=pi, t=t, hold=hold):
                        if t == 0:
                            hold["p"] = load_piece(d_winsg.ap()[2 + pi])
                        ap, nm = hold["p"]
                        b = nextbank()

                        def f(e):
                            ins = None
                            for k in range(16):
                                ins = e.matmul(psf[b][:, 0:256], lhsT=xTo[:, k, t * 128:(t + 1) * 128], rhs=ap[:, k, :],
                                               start=(k == 0), stop=(k == 15))
                            return ins
                        S.op("pe", f, reads=[nm, "xTo"], writes=[f"ps{b}"])
                        evac_copy(vg[:, t, pi * 256:(pi + 1) * 256], psf[b][:, 0:256], [f"ps{b}"], [f"vg{t}"])
                    out.append(g)
            return out

        load_head_w(1)
        load_T(0)
        for g in proj_groups(0):
            g()
        for h in range(8):
            if h + 2 < 8:
                load_head_w(h + 2)
            if h + 1 < 8:
                load_T(h + 1)
            plan_transposes(h)
            units = attn_units(h)
            if h + 1 < 8:
                pg = proj_groups(h + 1)
            else:
                S.alias([f"xTh{i}" for i in range(4)], ["uT", "memTb"] + [f"vg{t}" for t in range(8)])
                S.alias(["QT0", "KT0", "VT0"], ["ln_g", "ln_b", "bsbc", "wsTf", "trilf", "wsTm"])
                S.alias(["wsl0", "wsl1"], ["psl0", "psl1", "psl2"])
                S.dma("sp", [lambda e: e.dma_start(out=sgg[:, :], in_=bcast_rows(d_sgg, 512)),
                             lambda e: e.dma_start(out=sgb[:, :], in_=bcast_rows(d_sgb, 512)),
                             lambda e: e.dma_start(out=bsbc[:, :, :], in_=bcast_rows(d_bsp, 512).rearrange("p (g i) -> p g i", i=128)),
                             lambda e: e.dma_start(out=wsTf[:, :, :], in_=d_wsT.ap().rearrange("p (g i) -> p g i", i=128)),
                             lambda e: e.dma_start(out=trilf[:, :], in_=d_tril.ap())],
                      "s_c1", writes=["ln_g", "ln_b", "bsbc", "wsTf", "trilf"])
                S.dma("pool", cast_dma(memTb[:, :, :], d_memT.ap().rearrange("p (k m) -> p k m", m=256)), "s_memT", writes=["memTb"])
                pg = uv_groups()
            pgi = 0
            for i, (pS, pPV) in enumerate(units):
                pS()
                npg = 2 if i < 2 else 1
                for _ in range(npg):
                    if pgi < len(pg):
                        pg[pgi]()
                        pgi += 1
                if i >= 1:
                    units[i - 1][1]()
            units[-1][1]()
            while pgi < len(pg):
                pg[pgi]()
                pgi += 1
            S.op("act", lambda e: e.activation(out=rdh[:, :], in_=acc[:, 1, :], func=AF.Ln), reads=["acc"], writes=["st0", "st1"])
            S.op("act", lambda e: e.activation(out=rdh[:, :], in_=rdh[:, :], func=AF.Exp, scale=-1.0),
                 reads=["st0", "st1"], writes=["st0", "st1"])
            S.op("pool", lambda e, h=h: e.tensor_tensor(out=ocatT[:, h, :], in0=acc[:, 0, :], in1=rdh[:, :], op=ALU.mult),
                 reads=["acc", "st0", "st1"], writes=["ocatT"])
        S.op("dve", lambda e: e.tensor_tensor(out=wsTm[:, :, :], in0=wsTf[:, :, :],
                                              in1=trilf[:, :].unsqueeze(1).broadcast_to([128, 4, 128]), op=ALU.mult),
             reads=["wsTf", "trilf"], writes=["wsTm"])
        pre_k = [load_piece(d_wkv.ap()[0]), load_piece(d_wkv.ap()[1])]
        S.barrier()

        def load_wo(c):
            S.dma("pool", cast_dma(wo[c % 2][:, :, :], d_wout.ap()[c].rearrange("p (k c) -> p k c", c=256)), f"s_wo{c % 2}", writes=[f"wo{c % 2}"])

        def load_gu(f_):
            S.dma("pool", cast_dma(gu[f_ % 2][:, :, :], d_wgu.ap()[f_].rearrange("p (k c) -> p k c", c=256)), f"s_gu{f_ % 2}", writes=[f"gu{f_ % 2}"])


        for t in range(8):
            S.op("act", lambda e, t=t: e.activation(out=vg[:, t, :], in_=vg[:, t, :], func=AF.Gelu_apprx_tanh),
                 reads=[f"vg{t}"], writes=[f"vg{t}"])

        def gelu_u():
            for g in range(4):
                for tb in range(2):
                    S.op("act", lambda e, g=g, tb=tb: e.activation(out=uT[:, g, tb * 512:(tb + 1) * 512], in_=uT[:, g, tb * 512:(tb + 1) * 512],
                                                                 func=AF.Gelu_apprx_tanh),
                         reads=["uT"], writes=["uT"])
        later = []
        for pi in range(2):
            hold = {}
            for hh in range(2):
                def g(pi=pi, hh=hh, hold=hold):
                    if hh == 0:
                        hold["p"] = pre_k[pi]
                    ap, nm = hold["p"]
                    h = 2 * pi + hh
                    b = nextbank()

                    def f(e):
                        ins = None
                        for k in range(16):
                            ins = e.matmul(psf[b][:, 0:256], lhsT=ap[:, k, hh * 128:(hh + 1) * 128], rhs=memTb[:, k, :],
                                           start=(k == 0), stop=(k == 15))
                        return ins
                    S.op("pe", f, reads=[nm, "memTb"], writes=[f"ps{b}"])
                    S.op("act", lambda e: e.copy(out=KmT[:, h, :], in_=psf[b][:, 0:256]), reads=[f"ps{b}"], writes=["KmT"])
                later.append(g)
        for pi in range(2):
            hold = {}
            for mt in range(2):
                def g(pi=pi, mt=mt, hold=hold):
                    if mt == 0:
                        hold["p"] = load_piece(d_wkv.ap()[2 + pi])
                    ap, nm = hold["p"]
                    b = nextbank()

                    def f(e):
                        ins = None
                        for k in range(16):
                            ins = e.matmul(psf[b][:, 0:256], lhsT=memTb[:, k, mt * 128:(mt + 1) * 128], rhs=ap[:, k, :],
                                           start=(k == 0), stop=(k == 15))
                        return ins
                    S.op("pe", f, reads=[nm, "memTb"], writes=[f"ps{b}"])
                    S.op("act", lambda e: e.copy(out=Vm[:, mt, pi * 256:(pi + 1) * 256], in_=psf[b][:, 0:256]),
                         reads=[f"ps{b}"], writes=["Vm"])
                later.append(g)
        for pi in range(2):
            hold = {}
            for hh in range(2):
                for tb in range(2):
                    def g(pi=pi, hh=hh, tb=tb, hold=hold):
                        if hh == 0 and tb == 0:
                            hold["p"] = load_piece(d_winsg.ap()[4 + pi])
                        ap, nm = hold["p"]
                        h = 2 * pi + hh
                        b = nextbank()

                        def f(e):
                            ins = None
                            for k in range(16):
                                ins = e.matmul(psf[b][:, :], lhsT=ap[:, k, hh * 128:(hh + 1) * 128], rhs=xTo[:, k, tb * 512:(tb + 1) * 512],
                                               start=(k == 0), stop=(k == 15))
                            return ins
                        S.op("pe", f, reads=[nm, "xTo"], writes=[f"ps{b}"])
                        S.op("act", lambda e: e.mul(out=qmT[:, h, tb * 512:(tb + 1) * 512], in_=psf[b][:, :], mul=SCALE),
                             reads=[f"ps{b}"], writes=["qmT"])
                    later.append(g)
        assert len(later) == 16
        def sg_A(t):
            S.op("dve", lambda e: e.bn_stats(out=stats_b[:, t, :], in_=vg[:, t, :]), reads=[f"vg{t}"], writes=[f"lnst{t}"])
            ln_A(stats_b[:, t, :], mv_b[:, t, :], [f"lnst{t}"], f"lnmv{t}")

        def sg_B(t):
            ln_B(vg[:, t, :], mv_b[:, t, :], sgg[:, :], sgb[:, :], [f"vg{t}"], f"lnmv{t}", out_ap=vln[:, t, :], out_names=[f"vln{t}"])
        for t in range(8):
            sg_A(t)
            if t >= 1:
                sg_B(t - 1)
            later[2 * t]()
            later[2 * t + 1]()
        sg_B(7)
        gelu_u()
        load_wo(0)
        load_wo(1)
        pTm_b = [pTm, pTm2]
        rdm_b = [rdm, rdm2]

        def mixing_unit(i):
            g, half = i // 2, i % 2
            b = nextbank()
            pv = psf[b][:, :].rearrange("p (c i) -> p c i", i=128)

            def f(e):
                ins = None
                for ch in range(4):
                    ins = e.matmul(pv[:, ch, :], lhsT=vln[:, 4 * half + ch, g * 128:(g + 1) * 128], rhs=wsTm[:, g, :],
                                   start=True, stop=True)
                return ins
            S.op("pe", f, reads=[f"vln{4 * half + ch}" for ch in range(4)] + ["wsTm"], writes=[f"ps{b}"])
            S.op("dve", lambda e: e.tensor_tensor(out=tmpA[:, :, :], in0=pv,
                                                  in1=bsbc[:, g, :].unsqueeze(1).broadcast_to([128, 4, 128]), op=ALU.add),
                 reads=[f"ps{b}", "bsbc"], writes=["tmpA"])
            S.op("dve", lambda e: e.tensor_tensor(out=ocatT[:, 8 + g, half * 512:(half + 1) * 512],
                                                  in0=tmpA[:, :, :].rearrange("p c i -> p (c i)"),
                                                  in1=uT[:, g, half * 512:(half + 1) * 512], op=ALU.mult),
                 reads=["tmpA", "uT"], writes=["ocatT"])

        def mem_S(i):
            h, tb = i // 2, i % 2
            pb = pTm_b[i % 2]
            bsA, bsB = nextbank(), nextbank()
            q = qmT[:, h, tb * 512:(tb + 1) * 512]
            S.op("pe", lambda e: e.matmul(psf[bsA][:, :], lhsT=KmT[:, h, 0:128], rhs=q, start=True, stop=True),
                 reads=["KmT", "qmT"], writes=[f"ps{bsA}"])
            S.op("pe", lambda e: e.matmul(psf[bsB][:, :], lhsT=KmT[:, h, 128:256], rhs=q, start=True, stop=True),
                 reads=["KmT", "qmT"], writes=[f"ps{bsB}"])
            S.op("act", lambda e: e.activation(out=pb[:, 0, :], in_=psf[bsA][:, :], func=AF.Exp),
                 reads=[f"ps{bsA}"], writes=[f"pTm{i % 2}a"])
            S.op("act", lambda e: e.activation(out=pb[:, 1, :], in_=psf[bsB][:, :], func=AF.Exp),
                 reads=[f"ps{bsB}"], writes=[f"pTm{i % 2}b"])

        def mem_PV(i):
            h, tb = i // 2, i % 2
            pb = pTm_b[i % 2]
            rd = rdm_b[i % 2]
            pn = [f"pTm{i % 2}a", f"pTm{i % 2}b"]
            bn, bd = nextbank(), nextbank()

            def fn_(e):
                e.matmul(psf[bn][:, :], lhsT=Vm[:, 0, h * 128:(h + 1) * 128], rhs=pb[:, 0, :], start=True, stop=False)
                return e.matmul(psf[bn][:, :], lhsT=Vm[:, 1, h * 128:(h + 1) * 128], rhs=pb[:, 1, :], start=False, stop=True)

            def fd_(e):
                e.matmul(psf[bd][:, :], lhsT=vones[:, VOWN, :], rhs=pb[:, 0, :], start=True, stop=False)
                return e.matmul(psf[bd][:, :], lhsT=vones[:, VOWN, :], rhs=pb[:, 1, :], start=False, stop=True)
            S.op("pe", fn_, reads=["Vm"] + pn, writes=[f"ps{bn}"])
            S.op("pe", fd_, reads=["vones"] + pn, writes=[f"ps{bd}"])
            S.op("act", lambda e: e.activation(out=rd[:, :], in_=psf[bd][:, :], func=AF.Ln), reads=[f"ps{bd}"], writes=[f"rdm{i % 2}"])
            S.op("act", lambda e: e.activation(out=rd[:, :], in_=rd[:, :], func=AF.Exp, scale=-1.0),
                 reads=[f"rdm{i % 2}"], writes=[f"rdm{i % 2}"])
            S.op("dve", lambda e: e.tensor_tensor(out=ocatT[:, 12 + h, tb * 512:(tb + 1) * 512], in0=psf[bn][:, :],
                                                  in1=rd[:, :], op=ALU.mult),
                 reads=[f"ps{bn}", f"rdm{i % 2}"], writes=["ocatT"])

        mem_S(0)
        for i in range(8):
            mixing_unit(i)
            if i + 1 < 8:
                mem_S(i + 1)
            mem_PV(i)
        S.barrier()

        for t in range(8):
            S.dma("sp", lambda e, t=t: e.dma_start(out=x1buf[:, t, :], in_=d_xo.ap()[t * 128:(t + 1) * 128, :]), f"s_x{t}", writes=[f"x1_{t}"])
        S.dma("sp", [lambda e: e.dma_start(out=ln1g[:, :], in_=bcast_rows(d_ln1g, 2048)),
                     lambda e: e.dma_start(out=ln1b[:, :], in_=bcast_rows(d_ln1b, 2048))], "s_c2", writes=["ln_g", "ln_b"])
        load_gu(0)
        load_gu(1)

        def ln1_A(t):
            ln_A(stats_c[:, t, :, :].rearrange("p c s -> p (c s)"), mv_c[:, t, :], [f"lnst{t}"], f"lnmv{t}")

        def ln1_B(t):
            ln_B(x1buf[:, t, :], mv_c[:, t, :], ln1g[:, :], ln1b[:, :], [f"x1_{t}"], f"lnmv{t}")
            xb = xbs[t % 2]
            S.op("act", lambda e: e.copy(out=xb[:, :], in_=x1buf[:, t, :]), reads=[f"x1_{t}"], writes=[f"xb{t % 2}"])

        def tr1_tile(t):
            xb = xbs[t % 2]
            for half in range(2):
                bb = nextbbank()

                def f(e, bb=bb, half=half):
                    ins = None
                    for kk in range(8):
                        k = half * 8 + kk
                        ins = e.transpose(out=psb[bb][:, kk * 128:(kk + 1) * 128], in_=xb[:, k * 128:(k + 1) * 128], identity=identb[:, :])
                    return ins
                S.op("pe", f, reads=[f"xb{t % 2}", "identb"], writes=[f"pb{bb}"])
                S.op("act", lambda e, bb=bb, half=half: e.copy(out=x1T[:, half * 8:half * 8 + 8, t * 128:(t + 1) * 128],
                                                             in_=psb[bb][:, :].rearrange("p (k d) -> p k d", d=128)),
                     reads=[f"pb{bb}"], writes=["x1T"])

        for c in range(8):
            for t in range(8):
                b = nextbank()

                def f(e, b=b, c=c, t=t):
                    ins = None
                    for k in range(16):
                        ins = e.matmul(psf[b][:, 0:256], lhsT=ocatT[:, k, t * 128:(t + 1) * 128], rhs=wo[c % 2][:, k, :],
                                       start=(k == 0), stop=(k == 15))
                    return ins
                S.op("pe", f, reads=["ocatT", f"wo{c % 2}"], writes=[f"ps{b}"])
                S.op("dve", lambda e, b=b, c=c, t=t: e.scalar_tensor_tensor(out=x1buf[:, t, c * 256:(c + 1) * 256],
                                                                          in0=x1buf[:, t, c * 256:(c + 1) * 256], scalar=ALPHA,
                                                                          in1=psf[b][:, 0:256], op0=ALU.mult, op1=ALU.add),
                     reads=[f"ps{b}", f"x1_{t}"], writes=[f"x1_{t}"])
                S.op("dve", lambda e, c=c, t=t: e.bn_stats(out=stats_c[:, t, c, :], in_=x1buf[:, t, c * 256:(c + 1) * 256]),
                     reads=[f"x1_{t}"], writes=[f"lnst{t}"])
                if c == 7:
                    ln1_A(t)
                    if t >= 1:
                        ln1_B(t - 1)
                    if t >= 2:
                        tr1_tile(t - 2)
            if c + 2 < 8:
                load_wo(c + 2)
        ln1_B(7)
        tr1_tile(6)
        tr1_tile(7)
        S.barrier()

        for f_ in range(NF):
            w = gu[f_ % 2]
            wn = f"gu{f_ % 2}"
            for tb in range(2):
                bg, bu = nextbank(), nextbank()

                def fg(e, bg=bg, w=w, tb=tb):
                    ins = None
                    for k in range(16):
                        ins = e.matmul(psf[bg][:, :], lhsT=w[:, k, 0:128], rhs=x1T[:, k, tb * 512:(tb + 1) * 512], start=(k == 0), stop=(k == 15))
                    return ins

                def fu(e, bu=bu, w=w, tb=tb):
                    ins = None
                    for k in range(16):
                        ins = e.matmul(psf[bu][:, :], lhsT=w[:, k, 128:256], rhs=x1T[:, k, tb * 512:(tb + 1) * 512], start=(k == 0), stop=(k == 15))
                    return ins
                S.op("pe", fg, reads=[wn, "x1T"], writes=[f"ps{bg}"])
                S.op("pe", fu, reads=[wn, "x1T"], writes=[f"ps{bu}"])
                S.op("act", lambda e, bg=bg, tb=tb: e.activation(out=sgt[tb][:, :], in_=psf[bg][:, :], func=AF.Silu),
                     reads=[f"ps{bg}"], writes=[f"sgt{tb}"])
                S.op("dve", lambda e, bu=bu, tb=tb, f_=f_: e.tensor_tensor(out=hT[:, f_, tb * 512:(tb + 1) * 512], in0=psf[bu][:, :],
                                                                         in1=sgt[tb][:, :], op=ALU.mult),
                     reads=[f"ps{bu}", f"sgt{tb}"], writes=["hT"])
            if f_ + 2 < NF:
                load_gu(f_ + 2)
        S.barrier()

        def load_wd(c):
            S.dma("pool", [cast_dma(wdb[c % 2][:, 11 * i:11 * i + 11, :],
                                    d_wd.ap()[c].rearrange("p (k c) -> p k c", c=256)[:, 11 * i:11 * i + 11, :]) for i in range(4)],
                  f"s_wd{c % 2}", writes=[f"wd{c % 2}"])
        def ln2_A(t):
            ln_A(stats_d[:, t, :, :].rearrange("p c s -> p (c s)"), mv_d[:, t, :], [f"lnst{t}"], f"lnmv{t}")

        def ln2_B(t):
            ln_B(x1buf[:, t, :], mv_d[:, t, :], ln2g[:, :], ln2b[:, :], [f"x1_{t}"], f"lnmv{t}")
            S.dma("sp", lambda e: e.dma_start(out=d_out.ap()[t * 128:(t + 1) * 128, :], in_=x1buf[:, t, :]), "s_out",
                  reads=[f"x1_{t}"], is_output=True)
        load_wd(0)
        load_wd(1)
        for c in range(8):
            if c == 7:
                S.dma("sp", [lambda e: e.dma_start(out=ln2g[:, :], in_=bcast_rows(d_ln2g, 2048)),
                             lambda e: e.dma_start(out=ln2b[:, :], in_=bcast_rows(d_ln2b, 2048))], "s_c3", writes=["wd0", "ln_g", "ln_b"])
            for t in range(8):
                b = nextbank()

                def f(e, b=b, c=c, t=t):
                    ins = None
                    for k in range(NF):
                        ins = e.matmul(psf[b][:, 0:256], lhsT=hT[:, k, t * 128:(t + 1) * 128], rhs=wdb[c % 2][:, k, :],
                                       start=(k == 0), stop=(k == NF - 1))
                    return ins
                S.op("pe", f, reads=["hT", f"wd{c % 2}"], writes=[f"ps{b}"])
                S.op("dve", lambda e, b=b, c=c, t=t: e.scalar_tensor_tensor(out=x1buf[:, t, c * 256:(c + 1) * 256],
                                                                          in0=x1buf[:, t, c * 256:(c + 1) * 256], scalar=ALPHA,
                                                                          in1=psf[b][:, 0:256], op0=ALU.mult, op1=ALU.add),
                     reads=[f"ps{b}", f"x1_{t}"], writes=[f"x1_{t}"])
                S.op("dve", lambda e, c=c, t=t: e.bn_stats(out=stats_d[:, t, c, :], in_=x1buf[:, t, c * 256:(c + 1) * 256]),
                     reads=[f"x1_{t}"], writes=[f"lnst{t}"])
                if c == 7:
                    ln2_A(t)
                    if t >= 1:
                        ln2_B(t - 1)
            if c + 2 < 8:
                load_wd(c + 2)
        ln2_B(7)
        S.emit()
    return nc


def _tile_k(w):
    kd, c = w.shape
    return np.ascontiguousarray(w.reshape(kd // 128, 128, c).transpose(1, 0, 2)).reshape(128, (kd // 128) * c)


def _t5_bucket(dist):
    d = np.maximum(dist, 1).astype(np.float32)
    large = 16 + (np.log(d / np.float32(16)) / np.float32(math.log(2048 / 16)) * np.float32(16)).astype(np.int32)
    large = np.minimum(large, 31)
    return np.where(dist < 16, dist, large)


_NC_CACHE = {}


def kernel(x, mem, w_in, rel_bias, sg_ln_g, sg_ln_b, w_spatial, b_spatial, w_mem_kv, w_out,
           ln1_g, ln1_b, w_gate, w_up, w_down, ln2_g, ln2_b):
    f32 = np.float32
    x = np.asarray(x, f32)
    mem = np.asarray(mem, f32)
    w_in0 = np.asarray(w_in, f32)[0]
    wkv0 = np.asarray(w_mem_kv, f32)[0]
    wout0 = np.asarray(w_out, f32)[0]
    wg0 = np.asarray(w_gate, f32)[0]
    wu0 = np.asarray(w_up, f32)[0]
    wd0 = np.asarray(w_down, f32)[0]

    winh = np.stack([_tile_k(np.concatenate([w_in0[:, h * 128:(h + 1) * 128], w_in0[:, 1024 + h * 128:1024 + (h + 1) * 128],
                                             w_in0[:, 2048 + h * 128:2048 + (h + 1) * 128]], axis=1)) for h in range(8)])
    winsg = np.stack([_tile_k(w_in0[:, 3072 + i * 256:3072 + (i + 1) * 256]) for i in range(6)])
    wkv = np.stack([_tile_k(wkv0[:, i * 256:(i + 1) * 256]) for i in range(4)])
    wout = np.stack([_tile_k(wout0[:, i * 256:(i + 1) * 256]) for i in range(8)])
    wgu = np.stack([_tile_k(np.concatenate([wg0[:, f * 128:(f + 1) * 128], wu0[:, f * 128:(f + 1) * 128]], axis=1)) for f in range(NF)])
    wd = np.stack([_tile_k(wd0[:, i * 256:(i + 1) * 256]) for i in range(8)])
    wsT = np.ascontiguousarray(np.asarray(w_spatial, f32)[0].transpose(2, 0, 1)).reshape(128, 512)
    jj, ii = np.meshgrid(np.arange(128), np.arange(128), indexing="ij")
    tril = (jj <= ii).astype(f32)
    ident = np.eye(128, dtype=f32)
    oh = np.zeros((33, 3, 384), f32)
    for ri, r in enumerate((1, 4, 16)):
        for u in range(384):
            s = u - 127
            if 0 <= s <= 128:
                oh[int(_t5_bucket(np.array(s * r))), ri, u] = 1.0
            else:
                oh[32, ri, u] = 1.0
    shared = {
        "winh": winh, "winsg": winsg, "wkv": wkv, "wout": wout, "wgu": wgu, "wd": wd, "wsT": wsT, "tril": tril,
        "ident": ident, "bsp": np.asarray(b_spatial, f32)[0].reshape(1, 512),
        "sgg": np.asarray(sg_ln_g, f32).reshape(1, 512), "sgb": np.asarray(sg_ln_b, f32).reshape(1, 512),
        "ln1g": np.asarray(ln1_g, f32).reshape(1, 2048), "ln1b": np.asarray(ln1_b, f32).reshape(1, 2048),
        "ln2g": np.asarray(ln2_g, f32).reshape(1, 2048), "ln2b": np.asarray(ln2_b, f32).reshape(1, 2048),
        "rel": np.asarray(rel_bias, f32), "oh": oh.reshape(33, 3 * 384),
    }
    in_maps = []
    for core in range(8):
        b, q = core // 4, core % 4
        win = np.zeros((3072, 2048), f32)
        lo = 1024 * q - 2048
        src_lo = max(lo, 0)
        win[src_lo - lo:, :] = x[b, src_lo:1024 * q + 1024, :]
        wT = _tile_k(np.ascontiguousarray(win.T)).reshape(128, 16, 3072)
        valid = np.ones((128, 4), f32)
        valid[:, VR1H] = 1.0 if q >= 1 else 0.0
        valid[:, VR4H] = 1.0 if q >= 1 else 0.0
        thr = 128 - 64 * min(q, 2)
        valid[:, V16A] = (np.arange(128) >= thr).astype(f32)
        m = dict(shared)
        m["xTh"] = np.ascontiguousarray(wT[:, :, :2048]).reshape(128, 16 * 2048)
        m["xTo"] = np.ascontiguousarray(wT[:, :, 2048:]).reshape(128, 16 * 1024)
        m["xo"] = np.ascontiguousarray(x[b, 1024 * q:1024 * q + 1024, :])
        m["memT"] = _tile_k(np.ascontiguousarray(mem[b].T))
        m["valid"] = valid
        in_maps.append(m)

    if "nc" not in _NC_CACHE:
        _NC_CACHE["nc"] = build_program()
    nc = _NC_CACHE["nc"]
    res = run_bass_kernel_spmd(nc, in_maps, core_ids=list(range(8)))
    out = np.empty((2, 4096, 2048), f32)
    for core in range(8):
        b, q = core // 4, core % 4
        out[b, 1024 * q:1024 * q + 1024, :] = res.results[core]["out"]
    return out
```

```python
import contextlib
import math
import numpy as np
import concourse.bass as bass
import concourse.mybir as mybir
from concourse.bass_utils import run_bass_kernel_spmd

F32 = mybir.dt.float32
BF16 = mybir.dt.bfloat16
AF = mybir.ActivationFunctionType
ALU = mybir.AluOpType

K = 1024
D = 2048
NF = 44
ALPHA = float(2.0 ** 0.25)
SCALE = float(128.0 ** -0.5)
LN_EPS = 1e-5
TZROW = 128 * 385 + 128
SBUF_BASE = 16512
ENGS = ("pe", "act", "dve", "pool", "sp")
VOWN, VR1H, VR4H, V16A = 0, 1, 2, 3


class Sched:
    def __init__(self, nc, stack):
        self.nc = nc
        self.stack = stack
        self.prog = {e: [] for e in ENGS}
        self.sem_h = {}
        self.count = {}
        self.known = {e: {} for e in ENGS}
        self.res = {}
        for e in ("pe", "act", "dve", "pool"):
            self._sem("E_" + e)
        self.out_events = []

    def _sem(self, name):
        if name not in self.sem_h:
            self.sem_h[name] = self.stack.enter_context(self.nc.semaphore(name))
            self.count[name] = 0
        return self.sem_h[name]

    def _r(self, name):
        if name not in self.res:
            self.res[name] = {"w": None, "r": []}
        return self.res[name]

    def _waits(self, eng, reads, writes):
        need = {}

        def add(ev):
            if ev is None:
                return
            s, v = ev
            if need.get(s, 0) < v:
                need[s] = v

        for r in reads:
            add(self._r(r)["w"])
        for w in writes:
            rr = self._r(w)
            add(rr["w"])
            for ev in rr["r"]:
                add(ev)
        out = []
        for s, v in need.items():
            if self.known[eng].get(s, 0) >= v:
                continue
            self.known[eng][s] = v
            out.append((s, v))
        return out

    def _commit(self, ev, reads, writes):
        for r in reads:
            self._r(r)["r"].append(ev)
        for w in writes:
            rr = self._r(w)
            rr["w"] = ev
            rr["r"] = []

    def op(self, eng, fn, reads=(), writes=()):
        waits = self._waits(eng, reads, writes)
        s = "E_" + eng
        self.count[s] += 1
        ev = (s, self.count[s])
        self.prog[eng].append((waits, fn, s, 1))
        self._commit(ev, reads, writes)
        return ev

    def dma(self, queue, fns, sem, reads=(), writes=(), is_output=False):
        if not isinstance(fns, (list, tuple)):
            fns = [fns]
        self._sem(sem)
        waits = self._waits(queue, reads, writes)
        for i, fn in enumerate(fns):
            self.count[sem] += 16
            self.prog[queue].append((waits if i == 0 else [], fn, sem, 16))
        ev = (sem, self.count[sem])
        self._commit(ev, reads, writes)
        if is_output:
            self.out_events.append(ev)
        return ev

    def alias(self, old, new):
        evs = []
        for o in old:
            rr = self._r(o)
            if rr["w"] is not None:
                evs.append(rr["w"])
            evs.extend(rr["r"])
        for n in new:
            self._r(n)["r"].extend(evs)

    def barrier(self):
        snap = dict(self.count)
        for e in ENGS:
            waits = []
            for s, v in snap.items():
                if v > 0 and self.known[e].get(s, 0) < v:
                    self.known[e][s] = v
                    waits.append((s, v))
            self.prog[e].append((waits, None, None, 0))
        self.res = {}

    def emit(self):
        nc = self.nc
        fin = {}
        for s, v in self.out_events:
            fin[s] = max(fin.get(s, 0), v)
        with nc.Block() as block:
            def run(engname, eng):
                for waits, fn, s, inc in self.prog[engname]:
                    for ws, wv in waits:
                        eng.wait_ge(self.sem_h[ws], wv)
                    if fn is not None:
                        ins = fn(eng)
                        ins.then_inc(self.sem_h[s], inc)

            @block.tensor
            def _(e):
                run("pe", e)

            @block.scalar
            def _(e):
                run("act", e)

            @block.vector
            def _(e):
                run("dve", e)

            @block.gpsimd
            def _(e):
                run("pool", e)

            @block.sync
            def _(e):
                run("sp", e)
                for s, v in fin.items():
                    e.wait_ge(self.sem_h[s], v)


def build_program():
    nc = bass.Bass("TRN2", target_bir_lowering=False)

    def din(name, shape):
        return nc.dram_tensor(name, shape, F32, kind="ExternalInput")

    d_xTh = din("xTh", [128, 16 * 2048])
    d_xTo = din("xTo", [128, 16 * 1024])
    d_xo = din("xo", [1024, 2048])
    d_memT = din("memT", [128, 16 * 256])
    d_valid = din("valid", [128, 4])
    d_winh = din("winh", [8, 128, 16 * 384])
    d_winsg = din("winsg", [6, 128, 16 * 256])
    d_wkv = din("wkv", [4, 128, 16 * 256])
    d_wout = din("wout", [8, 128, 16 * 256])
    d_wgu = din("wgu", [NF, 128, 16 * 256])
    d_wd = din("wd", [8, 128, NF * 256])
    d_wsT = din("wsT", [128, 4 * 128])
    d_tril = din("tril", [128, 128])
    d_ident = din("ident", [128, 128])
    d_bsp = din("bsp", [1, 512])
    d_sgg = din("sgg", [1, 512])
    d_sgb = din("sgb", [1, 512])
    d_ln1g = din("ln1g", [1, 2048])
    d_ln1b = din("ln1b", [1, 2048])
    d_ln2g = din("ln2g", [1, 2048])
    d_ln2b = din("ln2b", [1, 2048])
    d_rel = din("rel", [32, 8])
    d_oh = din("oh", [33, 3 * 384])
    d_out = nc.dram_tensor("out", [1024, 2048], F32, kind="ExternalOutput")
    d_tz = nc.dram_tensor("tz", [1, 24 * TZROW], F32)

    def bcast_rows(handle, n):
        return bass.AP(handle, 0, [[0, 128], [1, n]])

    def A(name, shape, dt, off):
        return nc.alloc_sbuf_tensor_at(name, shape, dt, offset=SBUF_BASE + int(off))

    with contextlib.ExitStack() as st:
        S = Sched(nc, st)
        xTh = A("xTh", [128, 16, 2048], BF16, 0)
        xTo = A("xTo", [128, 16, 1024], BF16, 64 * K)
        ocatT = A("ocatT", [128, 16, 1024], BF16, 96 * K)
        wsl = [A("wsl0", [128, 16, 384], BF16, 128 * K), A("wsl1", [128, 16, 384], BF16, 140 * K)]
        psl = [A(f"psl{i}", [128, 16, 256], BF16, (128 + 8 * i) * K) for i in range(3)]
        QT = A("QT", [128, 1024], BF16, 152 * K)
        KT = A("KT", [128, 3072], BF16, 154 * K)
        VT = A("VT", [128, 3072], BF16, 160 * K)
        QT1 = A("QT1", [128, 1024], BF16, 112 * K)
        KT1 = A("KT1", [128, 3072], BF16, 114 * K)
        VT1 = A("VT1", [128, 3072], BF16, 120 * K)
        Varr = A("Varr", [128, 53, 128], BF16, 166 * K)
        Tt = [A("Tt0", [128, 3, 256], F32, 180 * K), A("Tt1", [128, 3, 256], F32, 183 * K)]
        acc = A("acc", [128, 2, 1024], F32, 186 * K)
        stt = [A("st0", [128, 512], F32, 194 * K), A("st1", [128, 512], F32, 196 * K)]
        rdh = A("rdh", [128, 1024], F32, 194 * K)
        pTt = [A("pT0", [128, 512], BF16, 198 * K), A("pT1", [128, 512], BF16, 199 * K)]
        pTm = A("pTm", [128, 2, 512], BF16, 198 * K)
        identb = A("identb", [128, 128], BF16, 200 * K)
        vones = A("vones", [128, 4, 128], BF16, 200 * K + 256)
        validf = A("validf", [128, 4], F32, 201 * K + 256)
        relx = A("relx", [33, 8], F32, 201 * K + 512)
        identf = A("identf", [128, 128], F32, 202 * K)
        Gs = A("Gs", [128, 24, 384], F32, 152 * K)
        relrep = A("relrep", [33, 8, 128], F32, 190 * K)
        ohs = A("ohs", [33, 3, 384], F32, 194 * K)
        uT = A("uT", [128, 4, 1024], F32, 0)
        vg = A("vg", [128, 8, 512], F32, 16 * K)
        vln = A("vln", [128, 8, 512], BF16, 32 * K)
        memTb = A("memTb", [128, 16, 256], BF16, 40 * K)
        KmT = A("KmT", [128, 4, 256], BF16, 48 * K)
        Vm = A("Vm", [128, 2, 512], BF16, 50 * K)
        qmT = A("qmT", [128, 4, 1024], BF16, 52 * K)
        sgg = A("sgg", [128, 512], F32, 152 * K)
        sgb = A("sgb", [128, 512], F32, 154 * K)
        bsbc = A("bsbc", [128, 4, 128], F32, 156 * K)
        wsTf = A("wsTf", [128, 4, 128], F32, 158 * K)
        wsTm = A("wsTm", [128, 4, 128], BF16, 160 * K)
        trilf = A("trilf", [128, 128], F32, 161 * K)
        stats_b = A("stats_b", [128, 8, 6], F32, 162 * K)
        mv_b = A("mv_b", [128, 8, 4], F32, 162 * K + 256)
        rdm = A("rdm", [128, 512], F32, 164 * K)
        tmpA = A("tmpA", [128, 4, 128], F32, 166 * K)
        pTm2 = A("pTm2", [128, 2, 512], BF16, 194 * K)
        rdm2 = A("rdm2", [128, 512], F32, 196 * K)
        x1buf = A("x1buf", [128, 8, 2048], F32, 0)
        x1T = A("x1T", [128, 16, 1024], BF16, 64 * K)
        wo = [A("wo0", [128, 16, 256], BF16, 168 * K), A("wo1", [128, 16, 256], BF16, 176 * K)]
        ln1g = A("ln1g", [128, 2048], F32, 144 * K)
        ln1b = A("ln1b", [128, 2048], F32, 152 * K)
        xbs = [A("xb0", [128, 2048], BF16, 160 * K), A("xb1", [128, 2048], BF16, 164 * K)]
        stats_c = A("stats_c", [128, 8, 8, 6], F32, 128 * K)
        mv_c = A("mv_c", [128, 8, 4], F32, 130 * K)
        hT = A("hT", [128, NF, 1024], BF16, 96 * K)
        gu = [A("gu0", [128, 16, 256], BF16, 184 * K), A("gu1", [128, 16, 256], BF16, 192 * K)]
        sgt = [A("sgt0", [128, 512], F32, 200 * K), A("sgt1", [128, 512], F32, 202 * K)]
        wdb = [A("wdA", [128, NF, 256], BF16, 64 * K), A("wdB", [128, NF, 256], BF16, 184 * K)]
        ln2g = A("ln2g", [128, 2048], F32, 64 * K)
        ln2b = A("ln2b", [128, 2048], F32, 72 * K)
        stats_d = A("stats_d", [128, 8, 8, 6], F32, 86 * K)
        mv_d = A("mv_d", [128, 8, 4], F32, 88 * K)

        psf = [st.enter_context(nc.psum_tensor(f"psf{i}", [128, 512], F32)) for i in range(6)]
        psb = [st.enter_context(nc.psum_tensor(f"psb{i}", [128, 1024], BF16)) for i in range(2)]
        bank_ctr = [0]
        bbank_ctr = [0]

        def nextbank():
            b = bank_ctr[0] % 6
            bank_ctr[0] += 1
            return b

        def nextbbank():
            b = bbank_ctr[0] % 2
            bbank_ctr[0] += 1
            return b

        evac_ctr = [0]
        st_ctr = [0]

        def evac_copy(out_ap, in_ap, reads, writes):
            evac_ctr[0] += 1
            if evac_ctr[0] % 2 == 0:
                S.op("act", lambda e: e.copy(out=out_ap, in_=in_ap), reads=reads, writes=writes)
            else:
                S.op("dve", lambda e: e.tensor_copy(out=out_ap, in_=in_ap), reads=reads, writes=writes)

        def cast_dma(out_ap, in_ap):
            return lambda e: e.dma_start(out=out_ap, in_=in_ap, max_dma_last_dim=4096)

        def ln_A(stats_flat, mv, stat_names, mn):
            S.op("dve", lambda e: e.bn_aggr(out=mv[:, 0:2], in_=stats_flat), reads=stat_names, writes=[mn])
            S.op("dve", lambda e: e.tensor_scalar_add(out=mv[:, 2:3], in0=mv[:, 1:2], scalar1=LN_EPS), reads=[mn], writes=[mn])
            S.op("act", lambda e: e.sqrt(out=mv[:, 2:3], in_=mv[:, 2:3]), reads=[mn], writes=[mn])
            S.op("dve", lambda e: e.reciprocal(out=mv[:, 2:3], in_=mv[:, 2:3]), reads=[mn], writes=[mn])

        def ln_B(buf_ap, mv, g_ap, b_ap, res_names, mn, out_ap=None, out_names=None):
            S.op("dve", lambda e: e.scalar_tensor_tensor(out=buf_ap, in0=buf_ap, scalar=mv[:, 0:1], in1=g_ap,
                                                         op0=ALU.subtract, op1=ALU.mult),
                 reads=list(res_names) + [mn, "ln_g"], writes=res_names)
            o = buf_ap if out_ap is None else out_ap
            wn = res_names if out_ap is None else out_names
            S.op("dve", lambda e: e.scalar_tensor_tensor(out=o, in0=buf_ap, scalar=mv[:, 2:3], in1=b_ap,
                                                         op0=ALU.mult, op1=ALU.add),
                 reads=list(res_names) + [mn, "ln_b"], writes=wn)

        S.dma("pool", cast_dma(wsl[0][:, :, :], d_winh.ap()[0].rearrange("p (k c) -> p k c", c=384)), "s_wsl0", writes=["wsl0"])
        S.dma("pool", [cast_dma(xTo[:, 8 * i:8 * i + 8, :], d_xTo.ap().rearrange("p (k t) -> p k t", t=1024)[:, 8 * i:8 * i + 8, :])
                       for i in range(2)], "s_xTo", writes=["xTo"])
        for tb in range(4):
            S.dma("pool", cast_dma(xTh[:, :, tb * 512:(tb + 1) * 512],
                                   d_xTh.ap().rearrange("p (k t) -> p k t", t=2048)[:, :, tb * 512:(tb + 1) * 512]),
                  f"s_xTh{tb}", writes=[f"xTh{tb}"])

        S.dma("sp", [lambda e: e.dma_start(out=identf[:, :], in_=d_ident.ap()),
                     lambda e: e.dma_start(out=validf[:, :], in_=d_valid.ap()),
                     lambda e: e.dma_start(out=relx[0:32, :], in_=d_rel.ap()),
                     lambda e: e.dma_start(out=ohs[:, :, :], in_=d_oh.ap().rearrange("p (r u) -> p r u", u=384))],
              "s_c0", writes=["identf", "validf", "relx", "ohs"])
        S.op("dve", lambda e: e.memset(relx[32:33, :], -30000.0), writes=["relx32"])
        S.op("dve", lambda e: e.tensor_copy(out=identb[:, :], in_=identf[:, :]), reads=["identf"], writes=["identb"])
        S.op("dve", lambda e: e.tensor_copy(out=vones[:, :, :], in_=validf[:, :].unsqueeze(2).broadcast_to([128, 4, 128])),
             reads=["validf"], writes=["vones"])
        S.op("dve", lambda e: e.tensor_copy(out=relrep[:, :, :], in_=relx[:, :].unsqueeze(2).broadcast_to([33, 8, 128])),
             reads=["relx", "relx32"], writes=["relrep"])
        for h in range(8):
            for r in range(3):
                idx = h * 3 + r
                b = nextbank()
                S.op("pe", lambda e, b=b, h=h, r=r: e.matmul(psf[b][:, 0:384], lhsT=relrep[:, h, :], rhs=ohs[:, r, :],
                                                          start=True, stop=True),
                     reads=["relrep", "ohs"], writes=[f"ps{b}"])
                evac_copy(Gs[:, idx, :], psf[b][:, 0:384], [f"ps{b}"], [f"Gs{idx}"])
        S.dma("sp", [(lambda e, idx=idx: e.dma_start(out=bass.AP(d_tz, idx * TZROW, [[385, 128], [1, 384]]), in_=Gs[:, idx, :]))
                     for idx in range(24)], "s_tz", reads=[f"Gs{i}" for i in range(24)], writes=["tz"])
        S.alias([f"Gs{i}" for i in range(24)] + ["relrep", "ohs"],
                ["QT0", "KT0", "VT0", "Varr", "Tt0", "Tt1", "acc", "st0", "st1", "pT0", "pT1"])

        def xT_block(k, tb):
            if tb < 4:
                return xTh[:, k, tb * 512:(tb + 1) * 512]
            return xTo[:, k, (tb - 4) * 512:(tb - 3) * 512]

        R1 = lambda j: j
        R4 = lambda c, m: 9 + 3 * c + m
        R16A = lambda c: 21 + c
        R16B = lambda c: 37 + c
        tile_cols = []
        for j in range(9):
            tile_cols.append((1920 + 128 * j, 1, 128))
        for c in range(4):
            for m in range(3):
                tile_cols.append((1536 + c + 512 * m, 4, 128))
        for c in range(16):
            tile_cols.append((c, 16, 128))
        for c in range(16):
            tile_cols.append((2048 + c, 16, 64))

        def cols(t, start, step, n):
            return t[:, start:start + step * (n - 1) + 1:step]

        QTs, KTs, VTs = [QT, QT1], [KT, KT1], [VT, VT1]

        def load_head_w(h):
            S.dma("pool", cast_dma(wsl[h % 2][:, :, :], d_winh.ap()[h].rearrange("p (k c) -> p k c", c=384)),
                  f"s_wsl{h % 2}", writes=[f"wsl{h % 2}"])

        def proj_groups(h):
            sl = wsl[h % 2]
            sln = f"wsl{h % 2}"
            hb = h % 2
            out = []

            def mk(which, tb):
                def g():
                    b = nextbank()

                    def f(e):
                        ins = None
                        for k in range(16):
                            ins = e.matmul(psf[b][:, :], lhsT=sl[:, k, which * 128:(which + 1) * 128], rhs=xT_block(k, tb),
                                           start=(k == 0), stop=(k == 15))
                        return ins
                    S.op("pe", f, reads=[sln, "xTo" if tb >= 4 else f"xTh{tb}"], writes=[f"ps{b}"])
                    if which == 0:
                        o = QTs[hb][:, (tb - 4) * 512:(tb - 3) * 512]
                        S.op("act", lambda e: e.mul(out=o, in_=psf[b][:, :], mul=SCALE), reads=[f"ps{b}"], writes=[f"QT{hb}"])
                    else:
                        dst = KTs[hb] if which == 1 else VTs[hb]
                        evac_copy(dst[:, tb * 512:(tb + 1) * 512], psf[b][:, :], [f"ps{b}"], [("KT" if which == 1 else "VT") + str(hb)])
                return g
            for which, tb in ((0, 4), (0, 5), (1, 4), (1, 5), (2, 4), (2, 5), (1, 0), (2, 0), (1, 1), (2, 1), (1, 2), (2, 2), (1, 3), (2, 3)):
                out.append(mk(which, tb))
            return out

        def plan_transposes(h):
            VTh = VTs[h % 2]
            vn = f"VT{h % 2}"
            for g0 in range(0, 53, 8):
                g1 = min(g0 + 8, 53)
                bb = nextbbank()

                def f(e, g0=g0, g1=g1, bb=bb):
                    ins = None
                    for ti in range(g0, g1):
                        s0, stp, n = tile_cols[ti]
                        ins = e.transpose(out=psb[bb][0:n, (ti - g0) * 128:(ti - g0 + 1) * 128], in_=cols(VTh, s0, stp, n),
                                          identity=identb[:, :])
                    return ins
                S.op("pe", f, reads=[vn, "identb"], writes=[f"pb{bb}"])
                nA = max(0, min(g1, 37) - g0)
                nB = (g1 - g0) - nA
                if nA > 0:
                    evac_copy(Varr[:, g0:g0 + nA, :], psb[bb][:, 0:nA * 128].rearrange("p (t d) -> p t d", d=128),
                              [f"pb{bb}"], ["Varr"])
                if nB > 0:
                    evac_copy(Varr[0:64, g0 + nA:g1, :], psb[bb][0:64, nA * 128:(g1 - g0) * 128].rearrange("p (t d) -> p t d", d=128),
                              [f"pb{bb}"], ["Varr"])

        def attn_units(h):
            hb = h % 2
            QTh, KTh = QTs[hb], KTs[hb]
            qn, kn = f"QT{hb}", f"KT{hb}"
            TT = Tt[hb]
            ttn = f"Tt{hb}"
            units = []

            def std_unit(kcur, kprev, qaps, vcur, vprev, vvalid_prev, acc_ap, first, r):
                ctx = {}

                def plan_S():
                    bs = nextbank()
                    sv = psf[bs][:, :].rearrange("p (b t q) -> p b t q", b=2, t=2, q=128)

                    def fs(e):
                        ins = None
                        for blk in range(2):
                            e.matmul(sv[:, blk, 0, :], lhsT=kcur[blk], rhs=qaps[blk], start=True, stop=True)
                            ins = e.matmul(sv[:, blk, 1, :], lhsT=kprev[blk], rhs=qaps[blk], start=True, stop=True)
                        return ins
                    S.op("pe", fs, reads=[kn, qn], writes=[f"ps{bs}"])
                    si = st_ctr[0] % 2
                    st_ctr[0] += 1
                    ctx["si"] = si
                    S.op("dve", lambda e: e.tensor_tensor(out=stt[si][:, :].rearrange("p (b j) -> p b j", b=2),
                                                          in0=psf[bs][:, :].rearrange("p (b j) -> p b j", b=2),
                                                          in1=TT[:, r, :].unsqueeze(1).broadcast_to([128, 2, 256]), op=ALU.add),
                         reads=[f"ps{bs}", ttn], writes=[f"st{si}"])
                    S.op("act", lambda e: e.activation(out=pTt[si][:, :], in_=stt[si][:, :], func=AF.Exp),
                         reads=[f"st{si}"], writes=[f"pT{si}"])

                def plan_PV():
                    si = ctx["si"]
                    bo = nextbank()
                    ov = psf[bo][:, :].rearrange("p (a q) -> p a q", a=2)

                    def fo(e):
                        ins = None
                        for blk in range(2):
                            pc = pTt[si][:, blk * 256:blk * 256 + 128]
                            pp = pTt[si][:, blk * 256 + 128:blk * 256 + 256]
                            e.matmul(ov[:, 0, blk * 128:(blk + 1) * 128], lhsT=vcur[blk], rhs=pc, start=True, stop=False)
                            e.matmul(ov[:, 0, blk * 128:(blk + 1) * 128], lhsT=vprev[blk], rhs=pp, start=False, stop=True)
                            e.matmul(ov[:, 1, blk * 128:(blk + 1) * 128], lhsT=vones[:, VOWN, :], rhs=pc, start=True, stop=False)
                            ins = e.matmul(ov[:, 1, blk * 128:(blk + 1) * 128], lhsT=vones[:, vvalid_prev[blk], :], rhs=pp,
                                           start=False, stop=True)
                        return ins
                    S.op("pe", fo, reads=[f"pT{si}", "Varr", "vones"], writes=[f"ps{bo}"])
                    if first:
                        evac_copy(acc_ap, ov, [f"ps{bo}"], ["acc"])
                    else:
                        S.op("dve", lambda e: e.tensor_tensor(out=acc_ap, in0=ov, in1=acc_ap, op=ALU.add),
                             reads=[f"ps{bo}", "acc"], writes=["acc"])
                units.append((plan_S, plan_PV))

            for pr in range(4):
                ns = (2 * pr, 2 * pr + 1)
                std_unit(
                    kcur=[KTh[:, 2048 + 128 * n:2048 + 128 * n + 128] for n in ns],
                    kprev=[KTh[:, 1920 + 128 * n:1920 + 128 * n + 128] for n in ns],
                    qaps=[QTh[:, 128 * n:128 * n + 128] for n in ns],
                    vcur=[Varr[:, R1(n + 1), :] for n in ns],
                    vprev=[Varr[:, R1(n), :] for n in ns],
                    vvalid_prev=[VR1H if n == 0 else VOWN for n in ns],
                    acc_ap=acc[:, :, 256 * pr:256 * pr + 256], first=True, r=0)
            for c in range(4):
                ns = (0, 1)
                std_unit(
                    kcur=[cols(KTh, 1536 + c + 512 * (n + 1), 4, 128) for n in ns],
                    kprev=[cols(KTh, 1536 + c + 512 * n, 4, 128) for n in ns],
                    qaps=[cols(QTh, c + 512 * n, 4, 128) for n in ns],
                    vcur=[Varr[:, R4(c, n + 1), :] for n in ns],
                    vprev=[Varr[:, R4(c, n), :] for n in ns],
                    vvalid_prev=[VR4H if n == 0 else VOWN for n in ns],
                    acc_ap=acc[:, :, c:c + 4 * 255 + 1:4], first=False, r=1)

            def r16_unit(cg):
                ctx = {}

                def plan_S():
                    bs = nextbank()
                    sv = psf[bs][:, :].rearrange("p (a c i) -> p a c i", a=2, c=4, i=64)

                    def fs(e):
                        ins = None
                        for ci in range(4):
                            c = 4 * cg + ci
                            q = cols(QTh, c, 16, 64)
                            e.matmul(sv[:, 0, ci, :], lhsT=cols(KTh, c, 16, 128), rhs=q, start=True, stop=True)
                            ins = e.matmul(sv[0:64, 1, ci, :], lhsT=cols(KTh, 2048 + c, 16, 64), rhs=q, start=True, stop=True)
                        return ins
                    S.op("pe", fs, reads=[kn, qn], writes=[f"ps{bs}"])
                    si = st_ctr[0] % 2
                    st_ctr[0] += 1
                    ctx["si"] = si
                    stv = stt[si][:, :].rearrange("p (a c i) -> p a c i", a=2, c=4, i=64)
                    ptv = pTt[si][:, :].rearrange("p (a c i) -> p a c i", a=2, c=4, i=64)
                    ctx["ptv"] = ptv
                    S.op("dve", lambda e: e.tensor_tensor(out=stv[:, 0, :, :], in0=sv[:, 0, :, :],
                                                          in1=TT[:, 2, 128:192].unsqueeze(1).broadcast_to([128, 4, 64]), op=ALU.add),
                         reads=[f"ps{bs}", ttn], writes=[f"st{si}"])
                    S.op("dve", lambda e: e.tensor_tensor(out=stv[0:64, 1, :, :], in0=sv[0:64, 1, :, :],
                                                          in1=TT[0:64, 2, 0:64].unsqueeze(1).broadcast_to([64, 4, 64]), op=ALU.add),
                         reads=[f"ps{bs}", ttn, f"st{si}"], writes=[f"st{si}"])
                    S.op("act", lambda e: e.activation(out=ptv[:, 0, :, :], in_=stv[:, 0, :, :], func=AF.Exp),
                         reads=[f"st{si}"], writes=[f"pT{si}"])
                    S.op("act", lambda e: e.activation(out=ptv[0:64, 1, :, :], in_=stv[0:64, 1, :, :], func=AF.Exp),
                         reads=[f"st{si}", f"pT{si}"], writes=[f"pT{si}"])

                def plan_PV():
                    si = ctx["si"]
                    ptv = ctx["ptv"]
                    bo = nextbank()
                    ov = psf[bo][:, :].rearrange("p (a c i) -> p a c i", a=2, c=4, i=64)

                    def fo(e):
                        ins = None
                        for ci in range(4):
                            c = 4 * cg + ci
                            e.matmul(ov[:, 0, ci, :], lhsT=Varr[:, R16A(c), :], rhs=ptv[:, 0, ci, :], start=True, stop=False)
                            e.matmul(ov[:, 0, ci, :], lhsT=Varr[0:64, R16B(c), :], rhs=ptv[0:64, 1, ci, :], start=False, stop=True)
                        e.matmul(ov[:, 1, :, :], lhsT=vones[:, V16A, :], rhs=ptv[:, 0, :, :], start=True, stop=False)
                        ins = e.matmul(ov[:, 1, :, :], lhsT=vones[0:64, VOWN, :], rhs=ptv[0:64, 1, :, :], start=False, stop=True)
                        return ins
                    S.op("pe", fo, reads=[f"pT{si}", "Varr", "vones"], writes=[f"ps{bo}"])
                    accv = acc[:, :, :].rearrange("p a (i c) -> p a c i", c=16)[:, :, 4 * cg:4 * cg + 4, :]
                    S.op("dve", lambda e: e.tensor_tensor(out=accv, in0=ov, in1=accv, op=ALU.add),
                         reads=[f"ps{bo}", "acc"], writes=["acc"])
                units.append((plan_S, plan_PV))
            for cg in range(4):
                r16_unit(cg)
            return units

        def load_T(h):
            TT = Tt[h % 2]
            S.dma("sp", [(lambda e, r=r: e.dma_start(out=TT[:, r, :],
                                                     in_=bass.AP(d_tz, (h * 3 + r) * TZROW + 127, [[384, 128], [1, 256]])))
                         for r in range(3)], f"s_tt{h % 2}", reads=["tz"], writes=[f"Tt{h % 2}"])

        sgsl = lambda i: (psl[i % 3][:, :, :], f"psl{i % 3}")
        piece_ctr = [0]

        def load_piece(src_ap):
            i = piece_ctr[0]
            piece_ctr[0] += 1
            ap, nm = sgsl(i)
            S.dma("pool", cast_dma(ap, src_ap.rearrange("p (k c) -> p k c", c=256)), f"s_{nm}", writes=[nm])
            return ap, nm

        def uv_groups():
            out = []
            for pi in range(2):
                hold = {}
                for gg in range(2):
                    for tb in range(2):
                        def g(pi=pi, gg=gg, tb=tb, hold=hold):
                            if gg == 0 and tb == 0:
                                hold["p"] = load_piece(d_winsg.ap()[pi])
                            ap, nm = hold["p"]
                            b = nextbank()

                            def f(e):
                                ins = None
                                for k in range(16):
                                    ins = e.matmul(psf[b][:, :], lhsT=ap[:, k, gg * 128:(gg + 1) * 128], rhs=xTo[:, k, tb * 512:(tb + 1) * 512],
                                                   start=(k == 0), stop=(k == 15))
                                return ins
                            S.op("pe", f, reads=[nm, "xTo"], writes=[f"ps{b}"])
                            evac_copy(uT[:, 2 * pi + gg, tb * 512:(tb + 1) * 512], psf[b][:, :], [f"ps{b}"], ["uT"])
                        out.append(g)
            for pi in range(2):
                hold = {}
                for t in range(8):
                    def g(pi=pi, t=t, hold=hold):
                        if t == 0:
                            hold["p"] = load_piece(d_winsg.ap()[2 + pi])
                        ap, nm = hold["p"]
                        b = nextbank()

                        def f(e):
                            ins = None
                            for k in range(16):
                                ins = e.matmul(psf[b][:, 0:256], lhsT=xTo[:, k, t * 128:(t + 1) * 128], rhs=ap[:, k, :],
                                               start=(k == 0), stop=(k == 15))
                            return ins
                        S.op("pe", f, reads=[nm, "xTo"], writes=[f"ps{b}"])
                        evac_copy(vg[:, t, pi * 256:(pi + 1) * 256], psf[b][:, 0:256], [f"ps{b}"], [f"vg{t}"])
                    out.append(g)
            return out

        load_head_w(1)
        load_T(0)
        for g in proj_groups(0):
            g()
        for h in range(8):
            if h + 2 < 8:
                load_head_w(h + 2)
            if h + 1 < 8:
                load_T(h + 1)
            plan_transposes(h)
            units = attn_units(h)
            if h + 1 < 8:
                pg = proj_groups(h + 1)
            else:
                S.alias([f"xTh{i}" for i in range(4)], ["uT", "memTb"] + [f"vg{t}" for t in range(8)])
                S.alias(["QT0", "KT0", "VT0"], ["ln_g", "ln_b", "bsbc", "wsTf", "trilf", "wsTm"])
                S.alias(["wsl0", "wsl1"], ["psl0", "psl1", "psl2"])
                S.dma("sp", [lambda e: e.dma_start(out=sgg[:, :], in_=bcast_rows(d_sgg, 512)),
                             lambda e: e.dma_start(out=sgb[:, :], in_=bcast_rows(d_sgb, 512)),
                             lambda e: e.dma_start(out=bsbc[:, :, :], in_=bcast_rows(d_bsp, 512).rearrange("p (g i) -> p g i", i=128)),
                             lambda e: e.dma_start(out=wsTf[:, :, :], in_=d_wsT.ap().rearrange("p (g i) -> p g i", i=128)),
                             lambda e: e.dma_start(out=trilf[:, :], in_=d_tril.ap())],
                      "s_c1", writes=["ln_g", "ln_b", "bsbc", "wsTf", "trilf"])
                S.dma("pool", cast_dma(memTb[:, :, :], d_memT.ap().rearrange("p (k m) -> p k m", m=256)), "s_memT", writes=["memTb"])
                pg = uv_groups()
            pgi = 0
            for i, (pS, pPV) in enumerate(units):
                pS()
                npg = 2 if i < 2 else 1
                for _ in range(npg):
                    if pgi < len(pg):
                        pg[pgi]()
                        pgi += 1
                if i >= 1:
                    units[i - 1][1]()
            units[-1][1]()
            while pgi < len(pg):
                pg[pgi]()
                pgi += 1
            S.op("act", lambda e: e.activation(out=rdh[:, :], in_=acc[:, 1, :], func=AF.Ln), reads=["acc"], writes=["st0", "st1"])
            S.op("act", lambda e: e.activation(out=rdh[:, :], in_=rdh[:, :], func=AF.Exp, scale=-1.0),
                 reads=["st0", "st1"], writes=["st0", "st1"])
            S.op("pool", lambda e, h=h: e.tensor_tensor(out=ocatT[:, h, :], in0=acc[:, 0, :], in1=rdh[:, :], op=ALU.mult),
                 reads=["acc", "st0", "st1"], writes=["ocatT"])
        S.op("dve", lambda e: e.tensor_tensor(out=wsTm[:, :, :], in0=wsTf[:, :, :],
                                              in1=trilf[:, :].unsqueeze(1).broadcast_to([128, 4, 128]), op=ALU.mult),
             reads=["wsTf", "trilf"], writes=["wsTm"])
        pre_k = [load_piece(d_wkv.ap()[0]), load_piece(d_wkv.ap()[1])]
        S.barrier()

        def load_wo(c):
            S.dma("pool", cast_dma(wo[c % 2][:, :, :], d_wout.ap()[c].rearrange("p (k c) -> p k c", c=256)), f"s_wo{c % 2}", writes=[f"wo{c % 2}"])

        def load_gu(f_):
            S.dma("pool", cast_dma(gu[f_ % 2][:, :, :], d_wgu.ap()[f_].rearrange("p (k c) -> p k c", c=256)), f"s_gu{f_ % 2}", writes=[f"gu{f_ % 2}"])


        for t in range(8):
            S.op("act", lambda e, t=t: e.activation(out=vg[:, t, :], in_=vg[:, t, :], func=AF.Gelu_apprx_tanh),
                 reads=[f"vg{t}"], writes=[f"vg{t}"])

        def gelu_u():
            for g in range(4):
                for tb in range(2):
                    S.op("act", lambda e, g=g, tb=tb: e.activation(out=uT[:, g, tb * 512:(tb + 1) * 512], in_=uT[:, g, tb * 512:(tb + 1) * 512],
                                                                 func=AF.Gelu_apprx_tanh),
                         reads=["uT"], writes=["uT"])
        later = []
        for pi in range(2):
            hold = {}
            for hh in range(2):
                def g(pi=pi, hh=hh, hold=hold):
                    if hh == 0:
                        hold["p"] = pre_k[pi]
                    ap, nm = hold["p"]
                    h = 2 * pi + hh
                    b = nextbank()

                    def f(e):
                        ins = None
                        for k in range(16):
                            ins = e.matmul(psf[b][:, 0:256], lhsT=ap[:, k, hh * 128:(hh + 1) * 128], rhs=memTb[:, k, :],
                                           start=(k == 0), stop=(k == 15))
                        return ins
                    S.op("pe", f, reads=[nm, "memTb"], writes=[f"ps{b}"])
                    S.op("act", lambda e: e.copy(out=KmT[:, h, :], in_=psf[b][:, 0:256]), reads=[f"ps{b}"], writes=["KmT"])
                later.append(g)
        for pi in range(2):
            hold = {}
            for mt in range(2):
                def g(pi=pi, mt=mt, hold=hold):
                    if mt == 0:
                        hold["p"] = load_piece(d_wkv.ap()[2 + pi])
                    ap, nm = hold["p"]
                    b = nextbank()

                    def f(e):
                        ins = None
                        for k in range(16):
                            ins = e.matmul(psf[b][:, 0:256], lhsT=memTb[:, k, mt * 128:(mt + 1) * 128], rhs=ap[:, k, :],
                                           start=(k == 0), stop=(k == 15))
                        return ins
                    S.op("pe", f, reads=[nm, "memTb"], writes=[f"ps{b}"])
                    S.op("act", lambda e: e.copy(out=Vm[:, mt, pi * 256:(pi + 1) * 256], in_=psf[b][:, 0:256]),
                         reads=[f"ps{b}"], writes=["Vm"])
                later.append(g)
        for pi in range(2):
            hold = {}
            for hh in range(2):
                for tb in range(2):
                    def g(pi=pi, hh=hh, tb=tb, hold=hold):
                        if hh == 0 and tb == 0:
                            hold["p"] = load_piece(d_winsg.ap()[4 + pi])
                        ap, nm = hold["p"]
                        h = 2 * pi + hh
                        b = nextbank()

                        def f(e):
                            ins = None
                            for k in range(16):
                                ins = e.matmul(psf[b][:, :], lhsT=ap[:, k, hh * 128:(hh + 1) * 128], rhs=xTo[:, k, tb * 512:(tb + 1) * 512],
                                               start=(k == 0), stop=(k == 15))
                            return ins
                        S.op("pe", f, reads=[nm, "xTo"], writes=[f"ps{b}"])
                        S.op("act", lambda e: e.mul(out=qmT[:, h, tb * 512:(tb + 1) * 512], in_=psf[b][:, :], mul=SCALE),
                             reads=[f"ps{b}"], writes=["qmT"])
                    later.append(g)
        assert len(later) == 16
        def sg_A(t):
            S.op("dve", lambda e: e.bn_stats(out=stats_b[:, t, :], in_=vg[:, t, :]), reads=[f"vg{t}"], writes=[f"lnst{t}"])
            ln_A(stats_b[:, t, :], mv_b[:, t, :], [f"lnst{t}"], f"lnmv{t}")

        def sg_B(t):
            ln_B(vg[:, t, :], mv_b[:, t, :], sgg[:, :], sgb[:, :], [f"vg{t}"], f"lnmv{t}", out_ap=vln[:, t, :], out_names=[f"vln{t}"])
        for t in range(8):
            sg_A(t)
            if t >= 1:
                sg_B(t - 1)
            later[2 * t]()
            later[2 * t + 1]()
        sg_B(7)
        gelu_u()
        load_wo(0)
        load_wo(1)
        pTm_b = [pTm, pTm2]
        rdm_b = [rdm, rdm2]

        def mixing_unit(i):
            g, half = i // 2, i % 2
            b = nextbank()
            pv = psf[b][:, :].rearrange("p (c i) -> p c i", i=128)

            def f(e):
                ins = None
                for ch in range(4):
                    ins = e.matmul(pv[:, ch, :], lhsT=vln[:, 4 * half + ch, g * 128:(g + 1) * 128], rhs=wsTm[:, g, :],
                                   start=True, stop=True)
                return ins
            S.op("pe", f, reads=[f"vln{4 * half + ch}" for ch in range(4)] + ["wsTm"], writes=[f"ps{b}"])
            S.op("dve", lambda e: e.tensor_tensor(out=tmpA[:, :, :], in0=pv,
                                                  in1=bsbc[:, g, :].unsqueeze(1).broadcast_to([128, 4, 128]), op=ALU.add),
                 reads=[f"ps{b}", "bsbc"], writes=["tmpA"])
            S.op("dve", lambda e: e.tensor_tensor(out=ocatT[:, 8 + g, half * 512:(half + 1) * 512],
                                                  in0=tmpA[:, :, :].rearrange("p c i -> p (c i)"),
                                                  in1=uT[:, g, half * 512:(half + 1) * 512], op=ALU.mult),
                 reads=["tmpA", "uT"], writes=["ocatT"])

        def mem_S(i):
            h, tb = i // 2, i % 2
            pb = pTm_b[i % 2]
            bsA, bsB = nextbank(), nextbank()
            q = qmT[:, h, tb * 512:(tb + 1) * 512]
            S.op("pe", lambda e: e.matmul(psf[bsA][:, :], lhsT=KmT[:, h, 0:128], rhs=q, start=True, stop=True),
                 reads=["KmT", "qmT"], writes=[f"ps{bsA}"])
            S.op("pe", lambda e: e.matmul(psf[bsB][:, :], lhsT=KmT[:, h, 128:256], rhs=q, start=True, stop=True),
                 reads=["KmT", "qmT"], writes=[f"ps{bsB}"])
            S.op("act", lambda e: e.activation(out=pb[:, 0, :], in_=psf[bsA][:, :], func=AF.Exp),
                 reads=[f"ps{bsA}"], writes=[f"pTm{i % 2}a"])
            S.op("act", lambda e: e.activation(out=pb[:, 1, :], in_=psf[bsB][:, :], func=AF.Exp),
                 reads=[f"ps{bsB}"], writes=[f"pTm{i % 2}b"])

        def mem_PV(i):
            h, tb = i // 2, i % 2
            pb = pTm_b[i % 2]
            rd = rdm_b[i % 2]
            pn = [f"pTm{i % 2}a", f"pTm{i % 2}b"]
            bn, bd = nextbank(), nextbank()

            def fn_(e):
                e.matmul(psf[bn][:, :], lhsT=Vm[:, 0, h * 128:(h + 1) * 128], rhs=pb[:, 0, :], start=True, stop=False)
                return e.matmul(psf[bn][:, :], lhsT=Vm[:, 1, h * 128:(h + 1) * 128], rhs=pb[:, 1, :], start=False, stop=True)

            def fd_(e):
                e.matmul(psf[bd][:, :], lhsT=vones[:, VOWN, :], rhs=pb[:, 0, :], start=True, stop=False)
                return e.matmul(psf[bd][:, :], lhsT=vones[:, VOWN, :], rhs=pb[:, 1, :], start=False, stop=True)
            S.op("pe", fn_, reads=["Vm"] + pn, writes=[f"ps{bn}"])
            S.op("pe", fd_, reads=["vones"] + pn, writes=[f"ps{bd}"])
            S.op("act", lambda e: e.activation(out=rd[:, :], in_=psf[bd][:, :], func=AF.Ln), reads=[f"ps{bd}"], writes=[f"rdm{i % 2}"])
            S.op("act", lambda e: e.activation(out=rd[:, :], in_=rd[:, :], func=AF.Exp, scale=-1.0),
                 reads=[f"rdm{i % 2}"], writes=[f"rdm{i % 2}"])
            S.op("dve", lambda e: e.tensor_tensor(out=ocatT[:, 12 + h, tb * 512:(tb + 1) * 512], in0=psf[bn][:, :],
                                                  in1=rd[:, :], op=ALU.mult),
                 reads=[f"ps{bn}", f"rdm{i % 2}"], writes=["ocatT"])

        mem_S(0)
        for i in range(8):
            mixing_unit(i)
            if i + 1 < 8:
                mem_S(i + 1)
            mem_PV(i)
        S.barrier()

        for t in range(8):
            S.dma("sp", lambda e, t=t: e.dma_start(out=x1buf[:, t, :], in_=d_xo.ap()[t * 128:(t + 1) * 128, :]), f"s_x{t}", writes=[f"x1_{t}"])
        S.dma("sp", [lambda e: e.dma_start(out=ln1g[:, :], in_=bcast_rows(d_ln1g, 2048)),
                     lambda e: e.dma_start(out=ln1b[:, :], in_=bcast_rows(d_ln1b, 2048))], "s_c2", writes=["ln_g", "ln_b"])

        def ln1_A(t):
            ln_A(stats_c[:, t, :, :].rearrange("p c s -> p (c s)"), mv_c[:, t, :], [f"lnst{t}"], f"lnmv{t}")

        def ln1_B(t):
            ln_B(x1buf[:, t, :], mv_c[:, t, :], ln1g[:, :], ln1b[:, :], [f"x1_{t}"], f"lnmv{t}")
            xb = xbs[t % 2]
            S.op("act", lambda e: e.copy(out=xb[:, :], in_=x1buf[:, t, :]), reads=[f"x1_{t}"], writes=[f"xb{t % 2}"])

        def tr1_tile(t):
            xb = xbs[t % 2]
            for half in range(2):
                bb = nextbbank()

                def f(e, bb=bb, half=half):
                    ins = None
                    for kk in range(8):
                        k = half * 8 + kk
                        ins = e.transpose(out=psb[bb][:, kk * 128:(kk + 1) * 128], in_=xb[:, k * 128:(k + 1) * 128], identity=identb[:, :])
                    return ins
                S.op("pe", f, reads=[f"xb{t % 2}", "identb"], writes=[f"pb{bb}"])
                S.op("act", lambda e, bb=bb, half=half: e.copy(out=x1T[:, half * 8:half * 8 + 8, t * 128:(t + 1) * 128],
                                                             in_=psb[bb][:, :].rearrange("p (k d) -> p k d", d=128)),
                     reads=[f"pb{bb}"], writes=["x1T"])

        def wo_group(c, t):
            b = nextbank()

            def f(e):
                ins = None
                for k in range(16):
                    ins = e.matmul(psf[b][:, 0:256], lhsT=ocatT[:, k, t * 128:(t + 1) * 128], rhs=wo[c % 2][:, k, :],
                                   start=(k == 0), stop=(k == 15))
                return ins
            S.op("pe", f, reads=["ocatT", f"wo{c % 2}"], writes=[f"ps{b}"])
            S.op("dve", lambda e: e.scalar_tensor_tensor(out=x1buf[:, t, c * 256:(c + 1) * 256],
                                                         in0=x1buf[:, t, c * 256:(c + 1) * 256], scalar=ALPHA,
                                                         in1=psf[b][:, 0:256], op0=ALU.mult, op1=ALU.add),
                 reads=[f"ps{b}", f"x1_{t}"], writes=[f"x1_{t}"])
            S.op("dve", lambda e: e.bn_stats(out=stats_c[:, t, c, :], in_=x1buf[:, t, c * 256:(c + 1) * 256]),
                 reads=[f"x1_{t}"], writes=[f"lnst{t}"])

        for c in range(6):
            for t in range(8):
                wo_group(c, t)
            load_wo(c + 2)
            if c == 1:
                load_gu(0)
                load_gu(1)
        for t in range(8):
            wo_group(6, t)
            wo_group(7, t)
            ln1_A(t)
            if t >= 1:
                ln1_B(t - 1)
            if t >= 2:
                tr1_tile(t - 2)
        ln1_B(7)
        tr1_tile(6)
        tr1_tile(7)
        S.barrier()

        for f_ in range(NF):
            w = gu[f_ % 2]
            wn = f"gu{f_ % 2}"
            for tb in range(2):
                bg, bu = nextbank(), nextbank()

                def fg(e, bg=bg, w=w, tb=tb):
                    ins = None
                    for k in range(16):
                        ins = e.matmul(psf[bg][:, :], lhsT=w[:, k, 0:128], rhs=x1T[:, k, tb * 512:(tb + 1) * 512], start=(k == 0), stop=(k == 15))
                    return ins

                def fu(e, bu=bu, w=w, tb=tb):
                    ins = None
                    for k in range(16):
                        ins = e.matmul(psf[bu][:, :], lhsT=w[:, k, 128:256], rhs=x1T[:, k, tb * 512:(tb + 1) * 512], start=(k == 0), stop=(k == 15))
                    return ins
                S.op("pe", fg, reads=[wn, "x1T"], writes=[f"ps{bg}"])
                S.op("pe", fu, reads=[wn, "x1T"], writes=[f"ps{bu}"])
                S.op("act", lambda e, bg=bg, tb=tb: e.activation(out=sgt[tb][:, :], in_=psf[bg][:, :], func=AF.Silu),
                     reads=[f"ps{bg}"], writes=[f"sgt{tb}"])
                S.op("dve", lambda e, bu=bu, tb=tb, f_=f_: e.tensor_tensor(out=hT[:, f_, tb * 512:(tb + 1) * 512], in0=psf[bu][:, :],
                                                                         in1=sgt[tb][:, :], op=ALU.mult),
                     reads=[f"ps{bu}", f"sgt{tb}"], writes=["hT"])
            if f_ + 2 < NF:
                load_gu(f_ + 2)
        S.barrier()

        def load_wd(c):
            S.dma("pool", [cast_dma(wdb[c % 2][:, 11 * i:11 * i + 11, :],
                                    d_wd.ap()[c].rearrange("p (k c) -> p k c", c=256)[:, 11 * i:11 * i + 11, :]) for i in range(4)],
                  f"s_wd{c % 2}", writes=[f"wd{c % 2}"])
        def ln2_A(t):
            ln_A(stats_d[:, t, :, :].rearrange("p c s -> p (c s)"), mv_d[:, t, :], [f"lnst{t}"], f"lnmv{t}")

        def ln2_B(t):
            ln_B(x1buf[:, t, :], mv_d[:, t, :], ln2g[:, :], ln2b[:, :], [f"x1_{t}"], f"lnmv{t}")
            S.dma("sp", lambda e: e.dma_start(out=d_out.ap()[t * 128:(t + 1) * 128, :], in_=x1buf[:, t, :]), "s_out",
                  reads=[f"x1_{t}"], is_output=True)
        load_wd(0)
        load_wd(1)
        for c in range(8):
            if c == 7:
                S.dma("sp", [lambda e: e.dma_start(out=ln2g[:, :], in_=bcast_rows(d_ln2g, 2048)),
                             lambda e: e.dma_start(out=ln2b[:, :], in_=bcast_rows(d_ln2b, 2048))], "s_c3", writes=["wd0", "ln_g", "ln_b"])
            for t in range(8):
                b = nextbank()

                def f(e, b=b, c=c, t=t):
                    ins = None
                    for k in range(NF):
                        ins = e.matmul(psf[b][:, 0:256], lhsT=hT[:, k, t * 128:(t + 1) * 128], rhs=wdb[c % 2][:, k, :],
                                       start=(k == 0), stop=(k == NF - 1))
                    return ins
                S.op("pe", f, reads=["hT", f"wd{c % 2}"], writes=[f"ps{b}"])
                S.op("dve", lambda e, b=b, c=c, t=t: e.scalar_tensor_tensor(out=x1buf[:, t, c * 256:(c + 1) * 256],
                                                                          in0=x1buf[:, t, c * 256:(c + 1) * 256], scalar=ALPHA,
                                                                          in1=psf[b][:, 0:256], op0=ALU.mult, op1=ALU.add),
                     reads=[f"ps{b}", f"x1_{t}"], writes=[f"x1_{t}"])
                S.op("dve", lambda e, c=c, t=t: e.bn_stats(out=stats_d[:, t, c, :], in_=x1buf[:, t, c * 256:(c + 1) * 256]),
                     reads=[f"x1_{t}"], writes=[f"lnst{t}"])
                if c == 7:
                    ln2_A(t)
                    if t >= 1:
                        ln2_B(t - 1)
            if c + 2 < 8:
                load_wd(c + 2)
        ln2_B(7)
        S.emit()
    return nc


def _tile_k(w):
    kd, c = w.shape
    return np.ascontiguousarray(w.reshape(kd // 128, 128, c).transpose(1, 0, 2)).reshape(128, (kd // 128) * c)


def _t5_bucket(dist):
    d = np.maximum(dist, 1).astype(np.float32)
    large = 16 + (np.log(d / np.float32(16)) / np.float32(math.log(2048 / 16)) * np.float32(16)).astype(np.int32)
    large = np.minimum(large, 31)
    return np.where(dist < 16, dist, large)


_NC_CACHE = {}


def kernel(x, mem, w_in, rel_bias, sg_ln_g, sg_ln_b, w_spatial, b_spatial, w_mem_kv, w_out,
           ln1_g, ln1_b, w_gate, w_up, w_down, ln2_g, ln2_b):
    f32 = np.float32
    x = np.asarray(x, f32)
    mem = np.asarray(mem, f32)
    w_in0 = np.asarray(w_in, f32)[0]
    wkv0 = np.asarray(w_mem_kv, f32)[0]
    wout0 = np.asarray(w_out, f32)[0]
    wg0 = np.asarray(w_gate, f32)[0]
    wu0 = np.asarray(w_up, f32)[0]
    wd0 = np.asarray(w_down, f32)[0]

    winh = np.stack([_tile_k(np.concatenate([w_in0[:, h * 128:(h + 1) * 128], w_in0[:, 1024 + h * 128:1024 + (h + 1) * 128],
                                             w_in0[:, 2048 + h * 128:2048 + (h + 1) * 128]], axis=1)) for h in range(8)])
    winsg = np.stack([_tile_k(w_in0[:, 3072 + i * 256:3072 + (i + 1) * 256]) for i in range(6)])
    wkv = np.stack([_tile_k(wkv0[:, i * 256:(i + 1) * 256]) for i in range(4)])
    wout = np.stack([_tile_k(wout0[:, i * 256:(i + 1) * 256]) for i in range(8)])
    wgu = np.stack([_tile_k(np.concatenate([wg0[:, f * 128:(f + 1) * 128], wu0[:, f * 128:(f + 1) * 128]], axis=1)) for f in range(NF)])
    wd = np.stack([_tile_k(wd0[:, i * 256:(i + 1) * 256]) for i in range(8)])
    wsT = np.ascontiguousarray(np.asarray(w_spatial, f32)[0].transpose(2, 0, 1)).reshape(128, 512)
    jj, ii = np.meshgrid(np.arange(128), np.arange(128), indexing="ij")
    tril = (jj <= ii).astype(f32)
    ident = np.eye(128, dtype=f32)
    oh = np.zeros((33, 3, 384), f32)
    for ri, r in enumerate((1, 4, 16)):
        for u in range(384):
            s = u - 127
            if 0 <= s <= 128:
                oh[int(_t5_bucket(np.array(s * r))), ri, u] = 1.0
            else:
                oh[32, ri, u] = 1.0
    shared = {
        "winh": winh, "winsg": winsg, "wkv": wkv, "wout": wout, "wgu": wgu, "wd": wd, "wsT": wsT, "tril": tril,
        "ident": ident, "bsp": np.asarray(b_spatial, f32)[0].reshape(1, 512),
        "sgg": np.asarray(sg_ln_g, f32).reshape(1, 512), "sgb": np.asarray(sg_ln_b, f32).reshape(1, 512),
        "ln1g": np.asarray(ln1_g, f32).reshape(1, 2048), "ln1b": np.asarray(ln1_b, f32).reshape(1, 2048),
        "ln2g": np.asarray(ln2_g, f32).reshape(1, 2048), "ln2b": np.asarray(ln2_b, f32).reshape(1, 2048),
        "rel": np.asarray(rel_bias, f32), "oh": oh.reshape(33, 3 * 384),
    }
    in_maps = []
    for core in range(8):
        b, q = core // 4, core % 4
        win = np.zeros((3072, 2048), f32)
        lo = 1024 * q - 2048
        src_lo = max(lo, 0)
        win[src_lo - lo:, :] = x[b, src_lo:1024 * q + 1024, :]
        wT = _tile_k(np.ascontiguousarray(win.T)).reshape(128, 16, 3072)
        valid = np.ones((128, 4), f32)
        valid[:, VR1H] = 1.0 if q >= 1 else 0.0
        valid[:, VR4H] = 1.0 if q >= 1 else 0.0
        thr = 128 - 64 * min(q, 2)
        valid[:, V16A] = (np.arange(128) >= thr).astype(f32)
        m = dict(shared)
        m["xTh"] = np.ascontiguousarray(wT[:, :, :2048]).reshape(128, 16 * 2048)
        m["xTo"] = np.ascontiguousarray(wT[:, :, 2048:]).reshape(128, 16 * 1024)
        m["xo"] = np.ascontiguousarray(x[b, 1024 * q:1024 * q + 1024, :])
        m["memT"] = _tile_k(np.ascontiguousarray(mem[b].T))
        m["valid"] = valid
        in_maps.append(m)

    if "nc" not in _NC_CACHE:
        _NC_CACHE["nc"] = build_program()
    nc = _NC_CACHE["nc"]
    res = run_bass_kernel_spmd(nc, in_maps, core_ids=list(range(8)))
    out = np.empty((2, 4096, 2048), f32)
    for core in range(8):
        b, q = core // 4, core % 4
        out[b, 1024 * q:1024 * q + 1024, :] = res.results[core]["out"]
    return out
```
